# Optimizing a Trainium2 kernel written in Bass

```python
import numpy as np
import jax
import jax.numpy as jnp
from jax import lax


D_MODEL = 1024
BATCH = 8
SEQ = 2048
DEPTH = 2

HEAD_DIM = 64
A_HEADS = D_MODEL // (4 * HEAD_DIM)
IDX_HEADS = 4
IDX_DIM = 64
DSA_TOPK = 256
B_HEADS = D_MODEL // (2 * HEAD_DIM)
B_WIDTH = B_HEADS * HEAD_DIM
DECAY_LORA = 64
AAA_LORA = 64
GATE_LORA = 128
GN_EPS = 64e-5
C_HEADS = D_MODEL // (4 * HEAD_DIM)
CMP_LEN = 32
CMP_STRIDE = 16
CMP_HID = 256
SEL_LEN = 64
SEL_TOPN = 16
WIN = 512
Q_BLOCK = 128
ROPE_THETA = 10000.0
NORM_EPS = 1e-6
NEG = -1e30
MIX_WIDTH = A_HEADS * HEAD_DIM + B_WIDTH + C_HEADS * HEAD_DIM
A_COLS = (A_HEADS * HEAD_DIM, HEAD_DIM, HEAD_DIM, IDX_HEADS * IDX_DIM, IDX_DIM, IDX_HEADS)
B_COLS = (B_WIDTH, B_WIDTH, B_WIDTH, DECAY_LORA, AAA_LORA, GATE_LORA)
C_COLS = (C_HEADS * HEAD_DIM, 6 * HEAD_DIM, 3 * C_HEADS)
IN_COLS = sum(A_COLS) + sum(B_COLS) + sum(C_COLS)
FFN_HIDDEN = ((8 * D_MODEL // 3 + 255) // 256) * 256

kernel_name = 'hymba_dsa_rwkv7_nsa_block'


def rms_norm(x, g, eps=NORM_EPS):
    xf = x.astype(jnp.float32)
    y = xf * lax.rsqrt(jnp.mean(xf * xf, axis=-1, keepdims=True) + eps)
    return (y * g.astype(jnp.float32)).astype(x.dtype)


def rope(x, pos):
    half = x.shape[-1] // 2
    inv = ROPE_THETA ** (-jnp.arange(half, dtype=jnp.float32) / half)
    ang = pos.astype(jnp.float32)[..., None] * inv
    cos, sin = jnp.cos(ang), jnp.sin(ang)
    xf = x.astype(jnp.float32)
    x1, x2 = xf[..., :half], xf[..., half:]
    return jnp.concatenate([x1 * cos - x2 * sin, x2 * cos + x1 * sin], axis=-1).astype(x.dtype)


def split_cols(z, sizes):
    return jnp.split(z, np.cumsum(sizes)[:-1].tolist(), axis=-1)


def attend(q, k, v, valid):
    s = jnp.einsum('bqhd,bqkd->bqhk', q, k, preferred_element_type=jnp.float32) * HEAD_DIM ** -0.5
    s = jnp.where(valid[:, :, None, :], s, -jnp.inf)
    p = jax.nn.softmax(s, axis=-1).astype(v.dtype)
    return jnp.einsum('bqhk,bqkd->bqhd', p, v)


def dsa_mixer(q, k, v, iq, ik, iw, q_g, k_g):
    B, T = q.shape[:2]
    pos = jnp.arange(T)
    q = rope(rms_norm(q, q_g), pos[:, None])
    k = rope(rms_norm(k, k_g), pos)
    iq = rope(iq, pos[:, None]).astype(jnp.float32)
    ik = rope(ik, pos).astype(jnp.float32)
    iw = iw.astype(jnp.float32)
    topk = min(DSA_TOPK, T // 4)

    def block(i):
        t0 = i * Q_BLOCK
        tq = t0 + jnp.arange(Q_BLOCK)
        qb = lax.dynamic_slice_in_dim(q, t0, Q_BLOCK, axis=1)
        iqb = lax.dynamic_slice_in_dim(iq, t0, Q_BLOCK, axis=1)
        iwb = lax.dynamic_slice_in_dim(iw, t0, Q_BLOCK, axis=1)
        score = jnp.einsum('bqh,bqhs->bqs', iwb,
                           jax.nn.relu(jnp.einsum('bqhd,bsd->bqhs', iqb, ik)))
        score = jnp.where(pos[None, None, :] <= tq[None, :, None], score, -jnp.inf)
        _, idx = lax.top_k(score, topk)
        ks = jax.vmap(lambda kb, ib: kb[ib])(k, idx)
        vs = jax.vmap(lambda vb, ib: vb[ib])(v, idx)
        return attend(qb, ks, vs, idx <= tq[None, :, None])

    o = lax.map(block, jnp.arange(T // Q_BLOCK))
    return jnp.moveaxis(o, 0, 1).reshape(B, T, A_HEADS * HEAD_DIM)


def rwkv7_mixer(zb, mu, w0, w2, a0, a2, g2, k_k, k_a, r_k, ln_w, ln_b):
    B, T, _ = zb.shape
    f32 = jnp.float32
    z_prev = jnp.pad(zb, ((0, 0), (1, 0), (0, 0)))[:, :-1]
    z = zb + (z_prev - zb) * mu
    r, k, v, wd, ad, gd = split_cols(z, B_COLS)
    w_log = -jax.nn.softplus(-(w0 + jnp.tanh(wd) @ w2).astype(f32)) - 0.5
    decay = jnp.exp(-jnp.exp(w_log))
    a = jax.nn.sigmoid((a0 + ad @ a2).astype(f32))
    g = jax.nn.sigmoid(gd) @ g2
    hs = lambda t: t.reshape(B, T, B_HEADS, HEAD_DIM)
    kk = hs((k * k_k).astype(f32))
    kk = kk / jnp.maximum(jnp.linalg.norm(kk, axis=-1, keepdims=True), 1e-12)
    k = hs(k.astype(f32) * (1.0 + (a - 1.0) * k_a))
    r, v, a, decay = hs(r.astype(f32)), hs(v.astype(f32)), hs(a), hs(decay)

    def step(S, inp):
        r_t, w_t, k_t, v_t, kk_t, a_t = inp
        sa = jnp.einsum('bhij,bhj->bhi', S, kk_t)
        S = (S * w_t[:, :, None, :] - sa[..., None] * (kk_t * a_t)[:, :, None, :]
             + v_t[..., None] * k_t[:, :, None, :])
        return S, jnp.einsum('bhij,bhj->bhi', S, r_t)

    xs = tuple(jnp.moveaxis(t, 1, 0) for t in (r, decay, k, v, kk, a))
    S0 = jnp.zeros((B, B_HEADS, HEAD_DIM, HEAD_DIM), f32)
    _, y = lax.scan(step, S0, xs)
    y = jnp.moveaxis(y, 0, 1)
    mean = jnp.mean(y, axis=-1, keepdims=True)
    var = jnp.mean((y - mean) ** 2, axis=-1, keepdims=True)
    y = ((y - mean) * lax.rsqrt(var + GN_EPS)).reshape(B, T, B_WIDTH) * ln_w + ln_b
    bonus = jnp.sum(r * k * r_k, axis=-1, keepdims=True) * v
    y = (y + bonus.reshape(B, T, B_WIDTH)) * g
    return y.astype(zb.dtype)


def nsa_mixer(q, kc, vc, ks, vs, kw, vw, gates, q_g, k_g, pe, w1, w2):
    B, T = q.shape[:2]
    f32 = jnp.float32
    pos = jnp.arange(T)
    scale = HEAD_DIM ** -0.5
    q = rope(rms_norm(q, q_g), pos[:, None])
    n_cmp = (T - CMP_LEN) // CMP_STRIDE + 1
    starts = np.arange(n_cmp) * CMP_STRIDE
    end_pos = starts + CMP_LEN - 1
    blk_idx = starts[:, None] + np.arange(CMP_LEN)[None, :]

    def compress(x, j):
        blk = x[:, blk_idx] + pe[j]
        h = jax.nn.gelu(blk.reshape(B, n_cmp, CMP_LEN * HEAD_DIM) @ w1[j])
        return h @ w2[j]

    k_cmp = rope(rms_norm(compress(kc, 0), k_g), jnp.asarray(end_pos))
    v_cmp = compress(vc, 1)
    s = jnp.einsum('bthd,bnd->bhtn', q, k_cmp, preferred_element_type=f32) * scale
    cmp_valid = jnp.asarray(end_pos)[None, :] <= pos[:, None]
    p = jax.nn.softmax(jnp.where(cmp_valid, s, NEG), axis=-1) * cmp_valid
    o_cmp = jnp.einsum('bhtn,bnd->bthd', p.astype(vc.dtype), v_cmp)
    n_blk = T // SEL_LEN
    sel_start = np.arange(n_blk) * SEL_LEN
    overlap = ((starts[:, None] <= sel_start[None, :] + SEL_LEN - 1)
               & (end_pos[:, None] >= sel_start[None, :])).astype(np.float32)
    imp = jnp.einsum('bhtn,nj->btj', p, jnp.asarray(overlap))
    blk = jnp.arange(n_blk)
    cur = pos // SEL_LEN
    admissible = blk[None, :] * SEL_LEN <= pos[:, None]
    forced = (blk[None, :] == 0) | (blk[None, :] == cur[:, None]) | (blk[None, :] == cur[:, None] - 1)
    imp = jnp.where(admissible, jnp.where(forced, jnp.inf, imp), -jnp.inf)
    top_n = min(SEL_TOPN, n_blk)
    _, sel = lax.top_k(imp, top_n)
    ks = rope(rms_norm(ks, k_g), pos)
    kw = rope(rms_norm(kw, k_g), pos)
    ks_blk = ks.reshape(B, n_blk, SEL_LEN, HEAD_DIM)
    vs_blk = vs.reshape(B, n_blk, SEL_LEN, HEAD_DIM)
    kw_pad = jnp.pad(kw, ((0, 0), (WIN, 0), (0, 0)))
    vw_pad = jnp.pad(vw, ((0, 0), (WIN, 0), (0, 0)))
    in_blk = jnp.arange(SEL_LEN)

    def block(i):
        t0 = i * Q_BLOCK
        tq = t0 + jnp.arange(Q_BLOCK)
        qb = lax.dynamic_slice_in_dim(q, t0, Q_BLOCK, axis=1)
        selb = lax.dynamic_slice_in_dim(sel, t0, Q_BLOCK, axis=1)
        kg = jax.vmap(lambda kb, ib: kb[ib])(ks_blk, selb).reshape(B, Q_BLOCK, top_n * SEL_LEN, HEAD_DIM)
        vg = jax.vmap(lambda vb, ib: vb[ib])(vs_blk, selb).reshape(B, Q_BLOCK, top_n * SEL_LEN, HEAD_DIM)
        kpos = (selb[..., None] * SEL_LEN + in_blk).reshape(B, Q_BLOCK, top_n * SEL_LEN)
        o_slc = attend(qb, kg, vg, kpos <= tq[None, :, None])
        kwb = lax.dynamic_slice_in_dim(kw_pad, t0, WIN + Q_BLOCK, axis=1)
        vwb = lax.dynamic_slice_in_dim(vw_pad, t0, WIN + Q_BLOCK, axis=1)
        wpos = t0 - WIN + jnp.arange(WIN + Q_BLOCK)
        wvalid = ((wpos[None, :] <= tq[:, None]) & (wpos[None, :] > tq[:, None] - WIN)
                  & (wpos[None, :] >= 0))
        sw = jnp.einsum('bqhd,bkd->bqhk', qb, kwb, preferred_element_type=f32) * scale
        sw = jnp.where(wvalid[None, :, None, :], sw, -jnp.inf)
        o_win = jnp.einsum('bqhk,bkd->bqhd', jax.nn.softmax(sw, axis=-1).astype(vwb.dtype), vwb)
        return o_slc, o_win

    o_slc, o_win = lax.map(block, jnp.arange(T // Q_BLOCK))
    unblock = lambda o: jnp.moveaxis(o, 0, 1).reshape(B, T, C_HEADS, HEAD_DIM)
    g = jax.nn.sigmoid(gates.astype(f32)).reshape(B, T, C_HEADS, 3)
    o = g[..., 0:1] * o_cmp + g[..., 1:2] * unblock(o_slc) + g[..., 2:3] * unblock(o_win)
    return o.reshape(B, T, C_HEADS * HEAD_DIM).astype(q.dtype)


def setup_inputs(seed: int = 0) -> dict:
    key = jax.random.key(seed)
    ks = iter(jax.random.split(key, 32))
    nrm = lambda shape, s: jax.random.normal(next(ks), shape, jnp.float32) * s
    L, D = DEPTH, D_MODEL
    return {
        'x': nrm((BATCH, SEQ, D), 1.0),
        'c': nrm((BATCH, D), 1.0),
        'ada_w': nrm((L, D, 6 * D), 0.5 * D ** -0.5),
        'ada_b': nrm((L, 6 * D), 0.01),
        'norm1_g': 1.0 + nrm((L, D), 0.02),
        'w_in': nrm((L, D, IN_COLS), D ** -0.5),
        'dsa_q_g': 1.0 + nrm((L, HEAD_DIM), 0.02),
        'dsa_k_g': 1.0 + nrm((L, HEAD_DIM), 0.02),
        'rwkv_mu': jax.random.uniform(next(ks), (L, sum(B_COLS)), jnp.float32),
        'rwkv_w0': nrm((L, B_WIDTH), 0.5),
        'rwkv_w2': nrm((L, DECAY_LORA, B_WIDTH), DECAY_LORA ** -0.5),
        'rwkv_a0': nrm((L, B_WIDTH), 0.1),
        'rwkv_a2': nrm((L, AAA_LORA, B_WIDTH), AAA_LORA ** -0.5),
        'rwkv_g2': nrm((L, GATE_LORA, B_WIDTH), GATE_LORA ** -0.5),
        'rwkv_k_k': 0.85 + nrm((L, B_WIDTH), 0.05),
        'rwkv_k_a': 1.0 + nrm((L, B_WIDTH), 0.05),
        'rwkv_r_k': nrm((L, B_HEADS, HEAD_DIM), 0.1),
        'rwkv_ln_w': 1.0 + nrm((L, B_WIDTH), 0.02),
        'rwkv_ln_b': nrm((L, B_WIDTH), 0.01),
        'nsa_q_g': 1.0 + nrm((L, HEAD_DIM), 0.02),
        'nsa_k_g': 1.0 + nrm((L, HEAD_DIM), 0.02),
        'nsa_pe': nrm((L, 2, CMP_LEN, HEAD_DIM), 0.02),
        'nsa_w1': nrm((L, 2, CMP_LEN * HEAD_DIM, CMP_HID), (CMP_LEN * HEAD_DIM) ** -0.5),
        'nsa_w2': nrm((L, 2, CMP_HID, HEAD_DIM), CMP_HID ** -0.5),
        'w_out': nrm((L, MIX_WIDTH, D), MIX_WIDTH ** -0.5),
        'norm2_g': 1.0 + nrm((L, D), 0.02),
        'ffn_wi': nrm((L, D, 2 * FFN_HIDDEN), D ** -0.5),
        'ffn_wo': nrm((L, FFN_HIDDEN, D), FFN_HIDDEN ** -0.5),
    }


def reference(x, c, ada_w, ada_b, norm1_g, w_in, dsa_q_g, dsa_k_g, rwkv_mu, rwkv_w0, rwkv_w2,
              rwkv_a0, rwkv_a2, rwkv_g2, rwkv_k_k, rwkv_k_a, rwkv_r_k, rwkv_ln_w, rwkv_ln_b,
              nsa_q_g, nsa_k_g, nsa_pe, nsa_w1, nsa_w2, w_out, norm2_g, ffn_wi, ffn_wo):
    B, T, _ = x.shape
    for l in range(DEPTH):
        mod = jax.nn.silu(c) @ ada_w[l] + ada_b[l]
        sh1, sc1, g1, sh2, sc2, g2 = jnp.split(mod[:, None, :], 6, axis=-1)
        h = rms_norm(x, norm1_g[l]) * (1.0 + sc1) + sh1
        z = h @ w_in[l]
        za, zb, zc = split_cols(z, (sum(A_COLS), sum(B_COLS), sum(C_COLS)))
        qa, ka, va, iq, ik, iw = split_cols(za, A_COLS)
        o_a = dsa_mixer(qa.reshape(B, T, A_HEADS, HEAD_DIM), ka, va,
                        iq.reshape(B, T, IDX_HEADS, IDX_DIM), ik, iw, dsa_q_g[l], dsa_k_g[l])
        o_b = rwkv7_mixer(zb, rwkv_mu[l], rwkv_w0[l], rwkv_w2[l], rwkv_a0[l], rwkv_a2[l], rwkv_g2[l],
                          rwkv_k_k[l], rwkv_k_a[l], rwkv_r_k[l], rwkv_ln_w[l], rwkv_ln_b[l])
        qc, kvc, gc = split_cols(zc, C_COLS)
        kc, vc, ksl, vsl, kwn, vwn = jnp.split(kvc, 6, axis=-1)
        o_c = nsa_mixer(qc.reshape(B, T, C_HEADS, HEAD_DIM), kc, vc, ksl, vsl, kwn, vwn, gc,
                        nsa_q_g[l], nsa_k_g[l], nsa_pe[l], nsa_w1[l], nsa_w2[l])
        mixed = jnp.concatenate([o_a, o_b, o_c], axis=-1) @ w_out[l]
        x = x + g1 * mixed
        h = rms_norm(x, norm2_g[l]) * (1.0 + sc2) + sh2
        gate, up = jnp.split(h @ ffn_wi[l], 2, axis=-1)
        x = x + g2 * ((jax.nn.silu(gate) * up) @ ffn_wo[l])
    return x
```

```python
import math
from contextlib import ExitStack

import numpy as np
import concourse.bass as bass
import concourse.mybir as mybir
from concourse.bass_utils import run_bass_kernel_spmd

F32 = mybir.dt.float32
BF16 = mybir.dt.bfloat16
F32R = mybir.dt.float32r


def RR(ap):
    return ap.bitcast(F32R)
AF = mybir.ActivationFunctionType
ALU = mybir.AluOpType
AX = mybir.AxisListType

D = 1024
T = 2048
NT = 16
DEPTH = 2
FFN_H = 2816
BIG = 32768.0
NEGF = -1.0e30


class Buf:
    def __init__(self, name, t, nsub=1):
        self.name = name
        self.t = t
        self.nsub = nsub

    def __getitem__(self, idx):
        return self.t[idx]


def keys_of(items):
    out = []
    for it in items:
        if isinstance(it, Buf):
            out.extend((it.name, s) for s in range(it.nsub))
        elif isinstance(it, tuple) and isinstance(it[0], Buf):
            b, s = it
            if isinstance(s, (list, tuple, range)):
                out.extend((b.name, x) for x in s)
            else:
                out.append((b.name, s))
        else:
            out.append(it)
    return out


class Op:
    __slots__ = ("eng", "fn", "reads", "writes", "dma", "idx", "signal", "sem", "val", "waits", "semkey")

    def __init__(self, eng, fn, reads, writes, dma):
        self.eng = eng
        self.fn = fn
        self.reads = reads
        self.writes = writes
        self.dma = dma
        self.signal = False
        self.sem = None
        self.val = 0
        self.waits = []
        self.semkey = None


class Prog:
    ENGS = ("pe", "act", "dve", "pool", "sp")

    def __init__(self, nc, sems, dma_sems):
        self.nc = nc
        self.sems = sems
        self.dma_sems = dma_sems
        self.ops = []
        self.cnt = {e: 0 for e in self.ENGS}
        self.dma_cum = [0] * len(dma_sems)
        self.dma_used = [False] * len(dma_sems)
        self.dma_rr = 0
        self.known = {e: {} for e in self.ENGS}
        self.n_ins = 0
        self.n_wait = 0

    def op(self, eng, fn, r=(), w=(), dma=False):
        o = Op(eng, fn, tuple(keys_of(r)), tuple(keys_of(w)), dma)
        o.idx = len(self.ops)
        self.ops.append(o)
        return o

    def flush(self):
        ops = self.ops
        self.ops = []
        if not ops:
            return
        fence = Op("sp", None, tuple(k for o in ops if o.dma for k in o.writes), (), False)
        fence.idx = len(ops)
        ops.append(fence)
        last_w = {}
        readers = {}
        deps_of = []
        for o in ops:
            deps = set()
            for k in o.reads:
                w = last_w.get(k)
                if w is not None:
                    deps.add(w)
            for k in o.writes:
                w = last_w.get(k)
                if w is not None:
                    deps.add(w)
                deps.update(readers.get(k, ()))
            deps.discard(o.idx)
            fd = []
            for d in deps:
                p = ops[d]
                if p.eng == o.eng and not p.dma:
                    if o.eng in ("pe", "sp"):
                        continue
                    if not (set(p.writes) & (set(o.reads) | set(o.writes))):
                        continue
                fd.append(d)
            deps_of.append(fd)
            for k in o.reads:
                readers.setdefault(k, []).append(o.idx)
            for k in o.writes:
                last_w[k] = o.idx
                readers[k] = []
        for fd in deps_of:
            for d in fd:
                ops[d].signal = True
        extra = {}
        nd = len(self.dma_sems)
        for o in ops:
            if o.dma:
                j = self.dma_rr % nd
                self.dma_rr += 1
                if self.dma_used[j]:
                    extra[o.idx] = (self.dma_sems[j], self.dma_cum[j], ("dma", j))
                self.dma_used[j] = True
                self.dma_cum[j] += 16
                o.sem = self.dma_sems[j]
                o.val = self.dma_cum[j]
                o.semkey = ("dma", j)
                o.signal = True
            elif o.signal:
                self.cnt[o.eng] += 1
                o.sem = self.sems[o.eng]
                o.val = self.cnt[o.eng]
                o.semkey = ("eng", o.eng)
        for o, fd in zip(ops, deps_of):
            need = {}
            for d in fd:
                p = ops[d]
                if need.get(p.semkey, (None, -1))[1] < p.val:
                    need[p.semkey] = (p.sem, p.val)
            if o.idx in extra:
                s, v, k = extra[o.idx]
                if need.get(k, (None, -1))[1] < v:
                    need[k] = (s, v)
            kn = self.known[o.eng]
            for k, (s, v) in need.items():
                if kn.get(k, -1) >= v:
                    continue
                kn[k] = v
                o.waits.append((s, v))
        by = {e: [o for o in ops if o.eng == e] for e in self.ENGS}
        self.n_ins += len(ops)
        self.n_wait += sum(len(o.waits) for o in ops)

        def run(engobj, lst):
            for o in lst:
                for s, v in o.waits:
                    engobj.wait_ge(s, v)
                if o.fn is None:
                    continue
                ins = o.fn(engobj)
                if o.signal:
                    ins.then_inc(o.sem, 16 if o.dma else 1)

        with self.nc.Block() as block:
            @block.tensor
            def _(e):
                run(e, by["pe"])

            @block.scalar
            def _(e):
                run(e, by["act"])

            @block.vector
            def _(e):
                run(e, by["dve"])

            @block.gpsimd
            def _(e):
                run(e, by["pool"])

            @block.sync
            def _(e):
                run(e, by["sp"])

    def dma(self, out, in_, r=(), w=(), q="sp"):
        return self.op(q, lambda e: e.dma_start(out=out, in_=in_), r, w, dma=True)

    def mm(self, out, lhsT, rhs, start=True, stop=True, r=(), w=(), skip=False):
        return self.op("pe", lambda e: e.matmul(out, lhsT=lhsT, rhs=rhs, start=start, stop=stop,
                                                skip_group_check=skip), r, w)

    def tr(self, out, in_, ident, r=(), w=()):
        return self.op("pe", lambda e: e.transpose(out, in_, ident), r, w)

    def act(self, out, in_, func, r=(), w=(), bias=None, scale=None):
        kw = {}
        if bias is not None:
            kw["bias"] = bias
        if scale is not None:
            kw["scale"] = scale
        return self.op("act", lambda e: e.activation(out=out, in_=in_, func=func, **kw), r, w)

    def cp(self, eng, out, in_, r=(), w=()):
        if eng == "act":
            return self.op("act", lambda e: e.copy(out=out, in_=in_), r, w)
        return self.op(eng, lambda e: e.tensor_copy(out=out, in_=in_), r, w)

    def tt(self, eng, out, in0, in1, op, r=(), w=()):
        return self.op(eng, lambda e: e.tensor_tensor(out=out, in0=in0, in1=in1, op=op), r, w)

    def ts(self, eng, out, in0, s1, op0, s2=None, op1=None, r=(), w=()):
        if op1 is None:
            return self.op(eng, lambda e: e.tensor_scalar(out=out, in0=in0, scalar1=s1, scalar2=None, op0=op0), r, w)
        return self.op(eng, lambda e: e.tensor_scalar(out=out, in0=in0, scalar1=s1, scalar2=s2, op0=op0, op1=op1), r, w)

    def stt(self, out, in0, scalar, in1, op0, op1, r=(), w=()):
        return self.op("dve", lambda e: e.scalar_tensor_tensor(out=out, in0=in0, scalar=scalar, in1=in1,
                                                               op0=op0, op1=op1), r, w)

    def red(self, out, in_, op, r=(), w=()):
        return self.op("dve", lambda e: e.tensor_reduce(out=out, in_=in_, axis=AX.X, op=op), r, w)

    def recip(self, out, in_, r=(), w=()):
        return self.op("dve", lambda e: e.reciprocal(out=out, in_=in_), r, w)

    def memset(self, eng, ap, val, w=()):
        return self.op(eng, lambda e: e.memset(ap, val), (), w)


def _rope_table(pos):
    half = 32
    inv = (np.float32(10000.0) ** (-np.arange(half, dtype=np.float32) / np.float32(half))).astype(np.float32)
    ang = pos.astype(np.float32)[:, None] * inv[None, :]
    return np.concatenate([np.cos(ang), np.sin(ang)], axis=1).astype(np.float32)


CF = {}


def _build_consts():
    parts = []
    off = 0

    def add(name, arr):
        nonlocal off
        a = np.zeros((128, arr.shape[1]), np.float32)
        a[: arr.shape[0]] = arr
        CF[name] = (off, arr.shape[1])
        parts.append(a)
        off += arr.shape[1]

    p = np.arange(128)
    add("ident", np.eye(128, dtype=np.float32))
    add("ones", np.ones((128, 128), np.float32))
    tab = _rope_table(np.arange(T))
    add("cs", tab.reshape(NT, 128, 64).transpose(1, 0, 2).reshape(128, NT * 64))
    up = (p[None, :] > p[:, None]).astype(np.float32)
    add("cneg", up * NEGF)
    add("cnegb", up * (-BIG))
    negu = (p[:, None] > p[None, :]).astype(np.float32) * (-BIG)
    negl = (p[:, None] <= p[None, :]).astype(np.float32) * (-BIG)
    add("negu4", np.tile(negu, (1, 4)))
    add("negl4", np.tile(negl, (1, 4)))
    add("i4", np.tile(np.eye(128, dtype=np.float32), (1, 4)))
    n = np.arange(127)
    add("cs_cmp", _rope_table(16 * n + 31))
    e = (np.arange(T)[None, :] // 64 == np.arange(32)[:, None]).astype(np.float32)
    add("efull", e)
    valid = (16 * n[:, None] + 31 <= np.arange(T)[None, :]).astype(np.float32)
    add("validT", valid)
    t = np.arange(T)
    blk = np.arange(32)
    cur = t // 64
    adm = blk[None, :] * 64 <= t[:, None]
    forced = (blk[None, :] == 0) | (blk[None, :] == cur[:, None]) | (blk[None, :] == cur[:, None] - 1)
    am = np.where(adm, np.where(forced, 1.0e30, 0.0), -1.0e30).astype(np.float32)
    add("addm", am.reshape(NT, 128, 32).transpose(1, 0, 2).reshape(128, NT * 32))
    starts = n * 16
    endp = starts + 31
    sel_start = blk * 64
    ov = ((starts[:, None] <= sel_start[None, :] + 63) & (endp[:, None] >= sel_start[None, :])).astype(np.float32)
    add("ovl", np.concatenate([np.ones((127, 1), np.float32), ov], axis=1))
    same = (p[:, None] // 64) == (p[None, :] // 64)
    incl = ((p[:, None] <= p[None, :]) & same).astype(np.float32)
    excl = ((p[:, None] < p[None, :]) & same).astype(np.float32)
    rev = ((p[:, None] > p[None, :]) & same).astype(np.float32)
    c = -math.exp(-0.5)
    add("tri3", np.concatenate([incl, excl, rev], axis=1) * np.float32(c))
    add("ms", excl)
    add("msl", rev)
    add("mi", incl)
    add("bd2", same.astype(np.float32))
    ind2 = np.zeros((128, 2), np.float32)
    ind2[:64, 0] = 1
    ind2[64:, 1] = 1
    add("ind2", ind2)
    add("i2", np.concatenate([np.eye(64, dtype=np.float32)] * 2, axis=0))
    return np.concatenate(parts, axis=1)


CONSTS = _build_consts()
NCF = CONSTS.shape[1]

VC = {"ada_b": (0, 48), "n1g": (48, 8), "n2g": (56, 8), "mu": (64, 14), "a0": (78, 4), "k_k": (82, 4),
      "k_a": (86, 4), "r_k": (90, 4), "pe": (94, 64)}
NVC = 158
RW = {"dsa_g": (0, 320), "nsa_qg": (320, 256), "nsa_kg": (576, 64), "ln_w": (640, 512), "ln_b": (1152, 512),
      "w0": (1664, 512)}
NRW = 2176


def _col(v):
    return np.ascontiguousarray(np.asarray(v, np.float32).reshape(-1, 128).T)


def build(layers=(0, 1), stages=("dsa", "rwkv", "nsa", "ffn"), dbg=False):
    nc = bass.Bass("TRN2", target_bir_lowering=False)
    dr = {}

    def din(name, shape):
        dr[name] = nc.dram_tensor(name, list(shape), F32, kind="ExternalInput").ap()
        return dr[name]

    x_d = din("x", [T, D])
    ccol_d = din("ccol", [128, 8])
    cf_d = din("cfd", [128, NCF])
    vec_d = din("vec", [DEPTH, 128, NVC])
    row_d = din("row", [DEPTH, NRW])
    ada_w = din("ada_w", [DEPTH, D, 6 * D])
    w_in = din("w_in", [DEPTH, D, 3152])
    w_out = din("w_out", [DEPTH, D, D])
    ffn_wi = din("ffn_wi", [DEPTH, D, 2 * FFN_H])
    ffn_wo = din("ffn_wo", [DEPTH, FFN_H, D])
    rw_w2 = din("rwkv_w2", [DEPTH, 64, 512])
    rw_a2 = din("rwkv_a2", [DEPTH, 64, 512])
    rw_g2 = din("rwkv_g2", [DEPTH, 128, 512])
    nsa_w1 = din("nsa_w1", [DEPTH, 2, 2048, 256])
    nsa_w2 = din("nsa_w2", [DEPTH, 2, 256, 64])
    y_d = nc.dram_tensor("y", [T, D], F32, kind="ExternalOutput").ap()
    if dbg:
        dbg_d = nc.dram_tensor("dbg", [128, 8192], F32, kind="ExternalOutput").ap()

    with ExitStack() as es:
        E = es.enter_context
        sems = {e: E(nc.semaphore("s_" + e)) for e in ("pe", "act", "dve", "pool")}
        dsems = [E(nc.semaphore(f"dq{i}")) for i in range(24)]
        P = Prog(nc, sems, dsems)

        uid = [0]

        def sb(stack, name, shape, dt=F32, nsub=1):
            uid[0] += 1
            name = f"{name}_{uid[0]}"
            return Buf(name, stack.enter_context(nc.sbuf_tensor("s_" + name, list(shape), dt)), nsub)

        xT = sb(es, "xT", [128, 8, T], F32, nsub=NT)
        hT = sb(es, "hT", [128, 8, T], BF16, nsub=NT)
        cf = sb(es, "cf", [128, 256], F32)
        identb = sb(es, "identb", [128, 128], BF16)
        onesr = sb(es, "onesr", [128, 128], F32)
        modT = sb(es, "modT", [128, 48], F32)
        A1 = sb(es, "A1", [128, 8], F32)
        A2 = sb(es, "A2", [128, 8], F32)
        vec = sb(es, "vec", [128, NVC], F32)
        ccol = sb(es, "ccol", [128, 8], F32)
        PS = [Buf(f"ps{i}", E(nc.psum_tensor(f"ps{i}", [128, 512], F32))) for i in range(8)]
        PSB = [p.t[:].bitcast(BF16) for p in PS]

        identf = cf[:, 0:128]
        onesf = cf[:, 128:256]

        def loadc(stack, name, dt=F32, rows=128, stage=None):
            o, wd = CF[name]
            t32 = sb(stack if (stage is None or dt == F32) else stage, "c32_" + name, [128, wd], F32)
            P.dma(t32[0:rows, :], cf_d[0:rows, o:o + wd], w=[t32])
            if dt == F32:
                return t32
            tb = sb(stack, "cb_" + name, [128, wd], dt)
            P.cp("pool", tb[0:rows, :], t32[0:rows, :], r=[t32], w=[tb])
            return tb

        def tkeys(buf, tg):
            return (buf, range(4 * tg, 4 * tg + 4))

        with ExitStack() as ph:
            xin = [sb(ph, f"xin{i}", [128, D]) for i in range(2)]
            P.dma(cf[:, :], cf_d[:, 0:256], w=[cf])
            P.dma(ccol[:, :], ccol_d, w=[ccol])
            P.cp("dve", identb[:, :], identf, r=[cf], w=[identb])
            P.cp("dve", RR(onesr[:, :]), onesf, r=[cf], w=[onesr])
            for tt in range(NT):
                xi = xin[tt % 2]
                P.dma(xi[:, :], x_d[tt * 128:(tt + 1) * 128, :], w=[xi])
                for half in range(2):
                    bank = PS[(tt * 2 + half) % 4]
                    for q in range(4):
                        fc = half * 4 + q
                        P.tr(bank[:, q * 128:(q + 1) * 128], xi[:, fc * 128:(fc + 1) * 128], identf,
                             r=[xi, cf], w=[bank])
                    P.cp("act" if half == 0 else "dve",
                         xT[:, half * 4:half * 4 + 4, tt * 128:(tt + 1) * 128],
                         bank[:, :].rearrange("p (a b) -> p a b", a=4), r=[bank], w=[(xT, tt)])
            P.flush()

        def rmsnorm_mod(Acol, shcol):
            with ExitStack() as ph:
                sq = [sb(ph, f"nsq{i}", [128, 512]) for i in range(2)]
                tmp = [sb(ph, f"ntmp{i}", [128, 512]) for i in range(2)]
                rstd = sb(ph, "nrstd", [128, 512])
                for tg in range(4):
                    sl = slice(tg * 512, (tg + 1) * 512)
                    bank = PS[tg % 2]
                    for fc in range(8):
                        s = sq[fc % 2]
                        P.act(RR(s[:, :]), xT[:, fc, sl], AF.Square, r=[tkeys(xT, tg)], w=[s])
                        P.mm(bank[:, :], RR(onesr[:, :]), RR(s[:, :]), start=(fc == 0), stop=(fc == 7), r=[s, onesr], w=[bank])
                    P.act(rstd[:, :], bank[:, :], AF.Sqrt, r=[bank], w=[rstd], bias=1e-6, scale=1.0 / D)
                    P.recip(rstd[:, :], rstd[:, :], r=[rstd], w=[rstd])
                    for fc in range(8):
                        tb = tmp[fc % 2]
                        P.tt("dve", tb[:, :], xT[:, fc, sl], rstd[:, :], ALU.mult, r=[tkeys(xT, tg), rstd], w=[tb])
                        P.act(hT[:, fc, sl], tb[:, :], AF.Identity, r=[tb, modT, A1, A2], w=[tkeys(hT, tg)],
                              bias=shcol(fc), scale=Acol(fc))
                P.flush()

        def outproj_add(srcT, nkc, wrow0, gcol0, l, tag):
            with ExitStack() as ph:
                wo32 = sb(ph, "wo32" + tag, [128, nkc, D])
                wo = sb(ph, "wo" + tag, [128, nkc, D], BF16)
                P.dma(wo32[:, :, :], w_out[l, wrow0:wrow0 + nkc * 128, :].rearrange("(k p) n -> p k n", p=128), w=[wo32])
                P.cp("pool", wo[:, :, :], wo32[:, :, :], r=[wo32], w=[wo])
                cnt = 0
                for tg in range(4):
                    sl = slice(tg * 512, (tg + 1) * 512)
                    for fc in range(8):
                        bank = PS[cnt % 4]
                        cnt += 1
                        for kc in range(nkc):
                            P.mm(bank[:, :], wo[:, kc, fc * 128:(fc + 1) * 128], srcT[:, kc, sl],
                                 start=(kc == 0), stop=(kc == nkc - 1), r=[wo, srcT], w=[bank])
                        P.stt(xT[:, fc, sl], bank[:, :], modT[:, gcol0 + fc:gcol0 + fc + 1], xT[:, fc, sl],
                              ALU.mult, ALU.add, r=[bank, modT, tkeys(xT, tg)], w=[tkeys(xT, tg)])
                P.flush()

        for l in layers:
            with ExitStack() as ph:
                awb = [sb(ph, f"awb{i}", [128, 8, 512]) for i in range(2)]
                sc = sb(ph, "silc", [128, 8, 2])
                P.dma(vec[:, :], vec_d[l], w=[vec])
                P.act(sc[:, :, 0], ccol[:, :], AF.Silu, r=[ccol], w=[sc])
                P.act(sc[:, :, 1], ccol[:, :], AF.Silu, r=[ccol], w=[sc])
                psA = PS[0]
                for nb in range(12):
                    ab = awb[nb % 2]
                    P.dma(ab[:, :, :], ada_w[l, :, nb * 512:(nb + 1) * 512].rearrange("(k p) n -> p k n", p=128), w=[ab])
                    for q in range(4):
                        j = nb * 4 + q
                        for kc in range(8):
                            P.mm(psA[:, 2 * j:2 * j + 2], ab[:, kc, q * 128:(q + 1) * 128], sc[:, kc, :],
                                 start=(kc == 0), stop=(kc == 7), r=[ab, sc], w=[psA], skip=True)
                P.tt("dve", modT[:, :], psA[:, 0:96].rearrange("p (j two) -> p j two", two=2)[:, :, 0],
                     vec[:, 0:48], ALU.add, r=[psA, vec], w=[modT])
                o1 = VC["n1g"][0]
                o2 = VC["n2g"][0]
                P.stt(A1[:, :], modT[:, 8:16], 1.0, vec[:, o1:o1 + 8], ALU.add, ALU.mult, r=[modT, vec], w=[A1])
                P.stt(A2[:, :], modT[:, 32:40], 1.0, vec[:, o2:o2 + 8], ALU.add, ALU.mult, r=[modT, vec], w=[A2])
                P.flush()

            rmsnorm_mod(lambda fc: A1[:, fc:fc + 1], lambda fc: modT[:, fc:fc + 1])
            if "dsa" in stages:
                dsa_phase(nc, P, l, locals())
            if "rwkv" in stages:
                rwkv_phase(nc, P, l, locals())
            if "nsa" in stages:
                nsa_phase(nc, P, l, locals())

            if "ffn" in stages:
                rmsnorm_mod(lambda fc: A2[:, fc:fc + 1], lambda fc: modT[:, 24 + fc:25 + fc])
                with ExitStack() as ph:
                    wi32 = [sb(ph, f"wi32{i}", [128, 8, 256]) for i in range(2)]
                    wib = [sb(ph, f"wib{i}", [128, 8, 256], BF16, nsub=2) for i in range(2)]
                    wo32 = sb(ph, "fwo32", [128, 22, 128])
                    wob = [sb(ph, f"fwob{i}", [128, 22, 128], BF16, nsub=2) for i in range(2)]
                    actT = sb(ph, "actT", [128, 22, 1024], BF16, nsub=44)
                    sg = [sb(ph, f"fsg{i}", [128, 512]) for i in range(2)]
                    cnt = 0
                    for tp in range(2):
                        for hc in range(22):
                            w32 = wi32[cnt % 2]
                            wb = wib[cnt % 2]
                            P.dma(w32[:, :, 0:128], ffn_wi[l, :, hc * 128:(hc + 1) * 128].rearrange("(k p) n -> p k n", p=128), w=[w32])
                            P.dma(w32[:, :, 128:256], ffn_wi[l, :, FFN_H + hc * 128:FFN_H + (hc + 1) * 128].rearrange("(k p) n -> p k n", p=128), w=[w32])
                            P.cp("pool", wb[:, 0:5, :], w32[:, 0:5, :], r=[w32], w=[(wb, 0)])
                            P.cp("act", wb[:, 5:8, :], w32[:, 5:8, :], r=[w32], w=[(wb, 1)])
                            for u in range(2):
                                tg = 2 * tp + u
                                sl = slice(tg * 512, (tg + 1) * 512)
                                bg = PS[(cnt % 2) * 4 + 2 * u]
                                bu = PS[(cnt % 2) * 4 + 2 * u + 1]
                                for kc in range(8):
                                    P.mm(bg[:, :], wb[:, kc, 0:128], hT[:, kc, sl], start=(kc == 0), stop=(kc == 7),
                                         r=[wb, tkeys(hT, tg)], w=[bg])
                                for kc in range(8):
                                    P.mm(bu[:, :], wb[:, kc, 128:256], hT[:, kc, sl], start=(kc == 0), stop=(kc == 7),
                                         r=[wb, tkeys(hT, tg)], w=[bu])
                                s_ = sg[u]
                                P.act(s_[:, :], bg[:, :], AF.Silu, r=[bg], w=[s_])
                                P.tt("dve", actT[:, hc, u * 512:(u + 1) * 512], s_[:, :], bu[:, :], ALU.mult, r=[s_, bu],
                                     w=[(actT, 2 * hc + u)])
                            cnt += 1
                        for fc in range(8):
                            wb = wob[fc % 2]
                            P.dma(wo32[:, :, :], ffn_wo[l, :, fc * 128:(fc + 1) * 128].rearrange("(k p) n -> p k n", p=128),
                                  r=[wo32], w=[wo32])
                            P.cp("pool", wb[:, 0:13, :], wo32[:, 0:13, :], r=[wo32], w=[(wb, 0)])
                            P.cp("act", wb[:, 13:22, :], wo32[:, 13:22, :], r=[wo32], w=[(wb, 1)])
                            for u in range(2):
                                tg = 2 * tp + u
                                sl = slice(tg * 512, (tg + 1) * 512)
                                bank = PS[(2 * fc + u) % 4]
                                for hc in range(22):
                                    P.mm(bank[:, :], wb[:, hc, :], actT[:, hc, u * 512:(u + 1) * 512], start=(hc == 0), stop=(hc == 21),
                                         r=[wb, (actT, 2 * hc + u)], w=[bank])
                                P.stt(xT[:, fc, sl], bank[:, :], modT[:, 40 + fc:41 + fc], xT[:, fc, sl],
                                      ALU.mult, ALU.add, r=[bank, modT, tkeys(xT, tg)], w=[tkeys(xT, tg)])
                    P.flush()

        with ExitStack() as ph:
            xo = [sb(ph, f"xo{i}", [128, D]) for i in range(2)]
            for tt in range(NT):
                xb = xo[tt % 2]
                for half in range(2):
                    bank = PS[(tt * 2 + half) % 4]
                    for q in range(4):
                        fc = half * 4 + q
                        P.tr(bank[:, q * 128:(q + 1) * 128], xT[:, fc, tt * 128:(tt + 1) * 128], identf,
                             r=[(xT, tt), cf], w=[bank])
                    P.cp("act" if half == 0 else "dve", xb[:, half * 512:(half + 1) * 512], bank[:, :],
                         r=[bank], w=[xb])
                P.dma(y_d[tt * 128:(tt + 1) * 128, :], xb[:, :], r=[xb], w=["y_out"])
            P.flush()
        build.stats = (P.n_ins, P.n_wait)
    return nc


def dsa_phase(nc, P, l, env):
    sb, xT, hT, PS, PSB, loadc = (env[k] for k in ("sb", "xT", "hT", "PS", "PSB", "loadc"))
    identb, w_in, row_d, modT, outproj_add = (env[k] for k in ("identb", "w_in", "row_d", "modT", "outproj_add"))
    NA = 708
    with ExitStack() as ph:
        oaT = sb(ph, "oaT", [128, 2, T], BF16, nsub=NT)
        with ExitStack() as p2:
            wA = sb(p2, "wA", [128, 8, NA], BF16)
            kT = sb(p2, "kT", [64, T], BF16, nsub=NT)
            ikT = sb(p2, "ikT", [64, T], BF16, nsub=NT)
            Vp = sb(p2, "Vp", [128, NT, 65], BF16, nsub=NT)
            qT2 = sb(p2, "qT2", [64, 2, 512], BF16, nsub=2)
            iqT2 = sb(p2, "iqT2", [64, 2, 512], BF16, nsub=2)
            zs = sb(p2, "zs", [128, NA])
            sqv = sb(p2, "sqv", [128, 320])
            ss = sb(p2, "ss", [128, 5])
            rw = sb(p2, "rw", [128, 10, 64])
            ro = sb(p2, "ro", [128, 10, 64], BF16)
            t1 = sb(p2, "t1", [128, 10, 32])
            t2 = sb(p2, "t2", [128, 10, 32])
            t3 = sb(p2, "t3", [128, 10, 32])
            t4 = sb(p2, "t4", [128, 10, 32])
            iwt = sb(p2, "iwt", [128, 2, 4], nsub=2)
            gA = sb(p2, "gA", [128, 320])
            sc = sb(p2, "sc", [128, T])
            wk = sb(p2, "wk", [128, T], BF16)
            lo = sb(p2, "lo", [128, 1])
            mid = sb(p2, "mid", [128, 1])
            cntb = sb(p2, "cntb", [128, 2])
            NBIS = 24
            nm = sb(p2, "nm", [128, 2, T], BF16, nsub=2)
            m8 = sb(p2, "m8", [128, 8])
            rl = sb(p2, "rl", [128, 2, 512], nsub=2)
            pT = sb(p2, "pT", [128, 2, 512], BF16, nsub=2)
            oa = sb(p2, "oa", [128, 256], BF16)
            rec = sb(p2, "rec", [128, 4])
            cs = loadc(p2, "cs")
            cneg = loadc(p2, "cneg")
            cnegb = loadc(p2, "cnegb", BF16)
            i4 = loadc(p2, "i4", BF16)
            with ExitStack() as p3:
                wst = sb(p3, "wAst", [128, 4, NA])
                for hf in range(2):
                    P.dma(wst[:, :, :], w_in[l, hf * 512:(hf + 1) * 512, 0:NA].rearrange("(k p) n -> p k n", p=128),
                          r=[wst], w=[wst])
                    P.cp("pool" if hf else "dve", wA[:, hf * 4:(hf + 1) * 4, :], wst[:, :, :], r=[wst], w=[wA])
                P.flush()
            o, w_ = RW["dsa_g"]
            P.dma(gA[:, :], row_d[l:l + 1, o:o + w_].partition_broadcast(128), w=[gA])
            P.memset("pool", Vp[:, :, :], 1.0, w=[Vp])

            import os
            LV = int(os.environ.get("KLV", "9"))

            def prep(i):
                tsl = slice(i * 128, (i + 1) * 128)
                if LV < 2:
                    return
                for kc in range(8):
                    P.mm(PS[0][:, 0:512], hT[:, kc, tsl], wA[:, kc, 0:512], start=(kc == 0), stop=(kc == 7),
                         r=[(hT, i), wA], w=[PS[0]])
                for kc in range(8):
                    P.mm(PS[1][:, 0:NA - 512], hT[:, kc, tsl], wA[:, kc, 512:NA], start=(kc == 0), stop=(kc == 7),
                         r=[(hT, i), wA], w=[PS[1]])
                P.cp("act", zs[:, 0:512], PS[0][:, 0:512], r=[PS[0]], w=[zs])
                P.cp("dve", zs[:, 512:NA], PS[1][:, 0:NA - 512], r=[PS[1]], w=[zs])
                if LV < 3:
                    return
                P.cp("act", Vp[:, i, 0:64], zs[:, 320:384], r=[zs], w=[(Vp, i)])
                P.cp("pool", iwt[:, i % 2, :], zs[:, 704:708], r=[zs], w=[(iwt, i % 2)])
                P.tt("dve", sqv[:, :], zs[:, 0:320], zs[:, 0:320], ALU.mult, r=[zs], w=[sqv])
                P.red(ss[:, :], sqv[:, :].rearrange("p (g d) -> p g d", g=5), ALU.add, r=[sqv], w=[ss])
                P.act(ss[:, :], ss[:, :], AF.Sqrt, r=[ss], w=[ss], bias=1e-6, scale=1.0 / 64)
                P.recip(ss[:, :], ss[:, :], r=[ss], w=[ss])
                P.tt("dve", rw[:, 0:5, :], zs[:, 0:320].rearrange("p (g d) -> p g d", g=5),
                     ss[:, :].unsqueeze(2).to_broadcast([128, 5, 64]), ALU.mult, r=[zs, ss], w=[rw])
                P.tt("dve", rw[:, 0:5, :], rw[:, 0:5, :], gA[:, :].rearrange("p (g d) -> p g d", g=5), ALU.mult,
                     r=[rw, gA], w=[rw])
                P.cp("pool", rw[:, 5:10, :], zs[:, 384:704].rearrange("p (g d) -> p g d", g=5), r=[zs], w=[rw])
                if LV < 4:
                    return
                cosb = cs[:, i * 64:i * 64 + 32].unsqueeze(1).to_broadcast([128, 10, 32])
                sinb = cs[:, i * 64 + 32:i * 64 + 64].unsqueeze(1).to_broadcast([128, 10, 32])
                P.tt("dve", t1[:, :, :], rw[:, :, 0:32], cosb, ALU.mult, r=[rw, cs], w=[t1])
                P.tt("pool", t2[:, :, :], rw[:, :, 32:64], sinb, ALU.mult, r=[rw, cs], w=[t2])
                P.tt("pool", t3[:, :, :], rw[:, :, 32:64], cosb, ALU.mult, r=[rw, cs], w=[t3])
                P.tt("dve", t4[:, :, :], rw[:, :, 0:32], sinb, ALU.mult, r=[rw, cs], w=[t4])
                P.tt("dve", ro[:, :, 0:32], t1[:, :, :], t2[:, :, :], ALU.subtract, r=[t1, t2], w=[ro])
                P.tt("pool", ro[:, :, 32:64], t3[:, :, :], t4[:, :, :], ALU.add, r=[t3, t4], w=[ro])
                if LV < 5:
                    return
                SUB = os.environ.get("KSUB", "qik")
                EV = os.environ.get("KEV", "dve")
                if "q" in SUB:
                    for g in range(4):
                        P.tr(PSB[0][0:64, g * 128:(g + 1) * 128], ro[:, g, :], identb[:, :], r=[ro, identb], w=[PS[0]])
                    P.cp(EV, qT2[:, i % 2, :], PSB[0][0:64, 0:512], r=[PS[0]], w=[(qT2, i % 2)])
                if "i" in SUB:
                    for g in range(4):
                        P.tr(PSB[0][0:64, 512 + g * 128:512 + (g + 1) * 128], ro[:, 5 + g, :], identb[:, :],
                             r=[ro, identb], w=[PS[0]])
                    P.cp("dve", iqT2[:, i % 2, :], PSB[0][0:64, 512:1024], r=[PS[0]], w=[(iqT2, i % 2)])
                if "k" in SUB:
                    P.tr(PSB[1][0:64, 0:128], ro[:, 4, :], identb[:, :], r=[ro, identb], w=[PS[1]])
                    P.tr(PSB[1][0:64, 128:256], ro[:, 9, :], identb[:, :], r=[ro, identb], w=[PS[1]])
                    P.cp(EV, kT[:, tsl], PSB[1][0:64, 0:128], r=[PS[1]], w=[(kT, i)])
                    P.cp("dve", ikT[:, tsl], PSB[1][0:64, 128:256], r=[PS[1]], w=[(ikT, i)])

            def select(i):
                L = (i + 1) * 128
                b = i % 2
                if i < 2:
                    if i == 1:
                        P.memset("pool", nm[:, b, 0:128], 0.0, w=[(nm, b)])
                    P.cp("pool", nm[:, b, i * 128:L], cnegb[:, :], r=[cnegb], w=[(nm, b)])
                    return
                cnt = 0
                for c0 in range(0, L, 512):
                    c1 = min(L, c0 + 512)
                    wd = c1 - c0
                    tiles = range(c0 // 128, c1 // 128)
                    for h in range(4):
                        bank = PS[2 + cnt % 2]
                        r_ = rl[:, cnt % 2, 0:wd]
                        P.mm(bank[:, 0:wd], iqT2[:, b, h * 128:(h + 1) * 128], ikT[:, c0:c1],
                             r=[(iqT2, b), (ikT, tiles)], w=[bank])
                        P.act(r_, bank[:, 0:wd], AF.Relu, r=[bank], w=[(rl, cnt % 2)])
                        if h == 0:
                            P.ts("dve", sc[:, c0:c1], r_, iwt[:, b, 0:1], ALU.mult, r=[(rl, cnt % 2), (iwt, b)], w=[sc])
                        else:
                            P.stt(sc[:, c0:c1], r_, iwt[:, b, h:h + 1], sc[:, c0:c1], ALU.mult, ALU.add,
                                  r=[(rl, cnt % 2), (iwt, b), sc], w=[sc])
                        cnt += 1
                P.tt("dve", sc[:, i * 128:L], sc[:, i * 128:L], cneg[:, :], ALU.add, r=[sc, cneg], w=[sc])
                P.memset("dve", mid[:, :], 0.0, w=[mid])
                for n_ in range(NBIS):
                    w_next = 2048.0 / (2 ** (n_ + 1))
                    P.op("dve", (lambda L_: lambda e: e.tensor_scalar(out=wk[:, 0:L_], in0=sc[:, 0:L_], scalar1=mid[:, 0:1],
                                                                      scalar2=0.0, op0=ALU.is_ge, op1=ALU.add,
                                                                      accum_out=cntb[:, 0:1]))(L), r=[sc, mid], w=[wk, cntb])
                    if n_ < NBIS - 1:
                        P.ts("dve", cntb[:, 1:2], cntb[:, 0:1], 255.5, ALU.is_ge, 2.0 * w_next, ALU.mult, r=[cntb], w=[cntb])
                        P.stt(mid[:, :], cntb[:, 1:2], -w_next, mid[:, :], ALU.add, ALU.add, r=[cntb, mid], w=[mid])
                    else:
                        w_last = 2048.0 / (2 ** n_)
                        P.ts("dve", cntb[:, 1:2], cntb[:, 0:1], 255.5, ALU.is_ge, -1.0, ALU.add, r=[cntb], w=[cntb])
                        P.stt(lo[:, :], cntb[:, 1:2], w_last, mid[:, :], ALU.mult, ALU.add, r=[cntb, mid], w=[lo])
                P.ts("dve", nm[:, b, 0:L], sc[:, 0:L], lo[:, 0:1], ALU.is_lt, -BIG, ALU.mult, r=[sc, lo], w=[(nm, b)])

            def attend(i):
                b = i % 2
                tsl = slice(i * 128, (i + 1) * 128)
                for j in range(i + 1):
                    bank = PS[2 + j % 2]
                    P.mm(bank[:, :], kT[:, j * 128:(j + 1) * 128], qT2[:, b, :], start=True, stop=False,
                         r=[(kT, j), (qT2, b)], w=[bank])
                    P.mm(bank[:, :], nm[:, b, j * 128:(j + 1) * 128], i4[:, :], start=False, stop=True,
                         r=[(nm, b), i4], w=[bank])
                    P.act(pT[:, j % 2, :], bank[:, :], AF.Exp, r=[bank], w=[(pT, j % 2)], scale=0.125)
                    for h in range(4):
                        P.mm(PS[4 + h][:, 0:65], pT[:, j % 2, h * 128:(h + 1) * 128], Vp[:, j, :],
                             start=(j == 0), stop=(j == i), r=[(pT, j % 2), (Vp, j)], w=[PS[4 + h]])
                for h in range(4):
                    P.recip(rec[:, h:h + 1], PS[4 + h][:, 64:65], r=[PS[4 + h]], w=[rec])
                    P.act(oa[:, h * 64:(h + 1) * 64], PS[4 + h][:, 0:64], AF.Copy, r=[PS[4 + h], rec], w=[oa],
                          scale=rec[:, h:h + 1])
                for c in range(2):
                    P.tr(PSB[1][:, 512 + c * 128:512 + (c + 1) * 128], oa[:, c * 128:(c + 1) * 128], identb[:, :],
                         r=[oa, identb], w=[PS[1]])
                P.cp("dve", oaT[:, :, tsl], PSB[1][:, 512:768].rearrange("p (c t) -> p c t", c=2), r=[PS[1]],
                     w=[(oaT, i)])

            import os
            mode = os.environ.get("KDBG", "full")
            if mode == "full":
                prep(0)
                select(0)
                for i in range(NT):
                    if i + 1 < NT:
                        prep(i + 1)
                        select(i + 1)
                    attend(i)
            elif mode == "prep":
                for i in range(NT):
                    prep(i)
            elif mode == "sel":
                for i in range(4):
                    prep(i)
                    select(i)
            elif mode == "att":
                for i in range(2):
                    prep(i)
                    select(i)
                    attend(i)
            P.flush()
        if mode == "full":
            outproj_add(oaT, 2, 0, 16, l, "A")


def rwkv_phase(nc, P, l, env):
    sb, xT, hT, PS, PSB, loadc = (env[k] for k in ("sb", "xT", "hT", "PS", "PSB", "loadc"))
    w_in, w_out, row_d, modT, vec, cf = (env[k] for k in ("w_in", "w_out", "row_d", "modT", "vec", "cf"))
    identf, onesf = env["identf"], env["onesf"]
    rw_w2, rw_a2, rw_g2 = env["rw_w2"], env["rw_a2"], env["rw_g2"]
    NB = 1792
    C0 = 708
    with ExitStack() as ph:
        wB = sb(ph, "wB", [128, 8, NB], BF16)
        woB = sb(ph, "woB", [128, 4, D], BF16)
        with ExitStack() as p3:
            wst = sb(p3, "wBst", [128, 2, NB])
            for q in range(4):
                P.dma(wst[:, :, :], w_in[l, q * 256:(q + 1) * 256, C0:C0 + NB].rearrange("(k p) n -> p k n", p=128),
                      r=[wst], w=[wst])
                P.cp("pool" if q % 2 else "dve", wB[:, 2 * q:2 * q + 2, :], wst[:, :, :], r=[wst], w=[wB])
            for q in range(2):
                P.dma(wst[:, :, 0:D], w_out[l, 256 + q * 256:256 + (q + 1) * 256, :].rearrange("(k p) n -> p k n", p=128),
                      r=[wst], w=[wst])
                P.cp("pool" if q % 2 else "dve", woB[:, 2 * q:2 * q + 2, :], wst[:, :, 0:D], r=[wst], w=[woB])
            P.flush()
        wa2 = sb(ph, "wa2", [128, 512])
        g2t = sb(ph, "g2t", [128, 512])
        w0r = sb(ph, "w0r", [1, 512])
        lnw = sb(ph, "lnw", [128, 512])
        lnb = sb(ph, "lnb", [128, 512])
        omka = sb(ph, "omka", [128, 4])
        zraw = sb(ph, "zraw", [128, 14, 129])
        zd = sb(ph, "zd", [128, 14, 128])
        twd = sb(ph, "twd", [64, 128])
        sgd = sb(ph, "sgd", [128, 128])
        sgw = sb(ph, "sgw", [128, 512])
        g_tm = sb(ph, "g_tm", [128, 512])
        tri3 = loadc(ph, "tri3")
        ms = loadc(ph, "ms")
        msl = loadc(ph, "msl")
        mi = loadc(ph, "mi")
        bd2 = loadc(ph, "bd2")
        ind2 = loadc(ph, "ind2")
        i2 = loadc(ph, "i2")
        def lane_bufs():
            d = {}
            d["ex3"] = sb(ph, "ex3", [128, 384])
            d["exn"] = sb(ph, "exn", [128, 128])
            d["aT"] = sb(ph, "aT", [128, 128])
            d["kkr"] = sb(ph, "kkr", [128, 128])
            d["sq"] = sb(ph, "rsq", [128, 128])
            d["rn"] = sb(ph, "rn", [128, 128])
            d["kk"] = sb(ph, "kk", [128, 128])
            d["tf"] = sb(ph, "tf", [128, 128])
            d["kp"] = sb(ph, "kp", [128, 128])
            d["bp"] = sb(ph, "bp", [128, 128])
            d["pr"] = sb(ph, "pr", [128, 128])
            d["AT"] = sb(ph, "AT", [128, 128])
            d["BT"] = sb(ph, "BT", [128, 128])
            d["KT"] = sb(ph, "KT", [128, 128])
            d["RT"] = sb(ph, "RT", [128, 128])
            d["KbT"] = sb(ph, "KbT", [128, 128])
            d["BbT"] = sb(ph, "BbT", [128, 128])
            d["tm"] = sb(ph, "tm", [128, 4, 128])
            d["Nm"] = sb(ph, "Nm", [128, 2, 128])
            d["Lm"] = sb(ph, "Lm", [128, 2, 128])
            d["S1"] = sb(ph, "S1", [128, 2, 128])
            d["NRB"] = sb(ph, "NRB", [128, 2, 128])
            d["X"] = sb(ph, "X", [128, 2, 128])
            d["RhT"] = sb(ph, "RhT", [128, 128])
            d["Y0s"] = sb(ph, "Y0s", [128, 128])
            d["GT"] = sb(ph, "GT", [128, 2, 64])
            d["Hs"] = sb(ph, "Hs", [128, 2, 64])
            return d
        LB = [lane_bufs(), None]
        with_core = LB[0]
        LB[1] = {k_: (v_ if k_ in ("Nm", "Lm", "S1", "NRB", "X", "RhT", "Y0s", "GT", "Hs") else None) for k_, v_ in with_core.items()}
        for k_ in list(LB[1]):
            if LB[1][k_] is None:
                shp = {"ex3": [128, 384], "tm": [128, 4, 128]}.get(k_, [128, 128])
                LB[1][k_] = sb(ph, k_ + "_l1", shp)
        NAMES = ['ex3', 'exn', 'aT', 'kkr', 'sq', 'rn', 'kk', 'tf', 'kp', 'bp', 'pr', 'AT', 'BT', 'KT', 'RT', 'KbT', 'BbT', 'tm', 'Nm', 'Lm', 'S1', 'NRB', 'X', 'RhT', 'Y0s', 'GT', 'Hs']
        Mst = sb(ph, "Mst", [128, 4, 64], nsub=4)
        y_tm = sb(ph, "y_tm", [128, 512], nsub=4)
        vtm = sb(ph, "vtm", [128, 512], nsub=4)
        ysq = sb(ph, "ysq", [128, 512])
        st = sb(ph, "gnst", [128, 4, 8])
        bon = sb(ph, "bon", [128, 8])
        obT = sb(ph, "obT", [128, 4, 128], BF16)

        P.dma(wa2[0:64, :], rw_w2[l], w=[wa2])
        P.dma(wa2[64:128, :], rw_a2[l], w=[wa2])
        P.dma(g2t[:, :], rw_g2[l], w=[g2t])
        o, w_ = RW["w0"]
        P.dma(w0r[0:1, :], row_d[l:l + 1, o:o + w_], w=[w0r])
        o, w_ = RW["ln_w"]
        P.dma(lnw[:, :], row_d[l:l + 1, o:o + w_].partition_broadcast(128), w=[lnw])
        o, w_ = RW["ln_b"]
        P.dma(lnb[:, :], row_d[l:l + 1, o:o + w_].partition_broadcast(128), w=[lnb])
        oka = VC["k_a"][0]
        okk = VC["k_k"][0]
        oa0 = VC["a0"][0]
        ork = VC["r_k"][0]
        omu = VC["mu"][0]
        P.ts("dve", omka[:, :], vec[:, oka:oka + 4], -1.0, ALU.mult, 1.0, ALU.add, r=[vec], w=[omka])
        P.memset("pool", zraw[:, :, :], 0.0, w=[zraw])
        P.memset("pool", Mst[:, :, :], 0.0, w=[Mst])
        mub = vec[:, omu:omu + 14].unsqueeze(2).to_broadcast([128, 14, 128])
        rot = [0]

        def nb():
            rot[0] += 1
            return PS[rot[0] % 4]

        import os
        RL = int(os.environ.get("KRL", "9"))
        for i in range(NT):
            tsl = slice(i * 128, (i + 1) * 128)
            if RL < 2:
                continue
            for f in range(14):
                bank = PS[f // 4]
                for kc in range(8):
                    P.mm(bank[:, (f % 4) * 128:(f % 4 + 1) * 128], wB[:, kc, f * 128:(f + 1) * 128], hT[:, kc, tsl],
                         start=(kc == 0), stop=(kc == 7), r=[wB, (hT, i)], w=[bank])
            for b4 in range(4):
                n_ = 4 if b4 < 3 else 2
                P.cp("act" if b4 % 2 == 0 else "dve", zraw[:, b4 * 4:b4 * 4 + n_, 1:129],
                     PS[b4][:, 0:n_ * 128].rearrange("p (a b) -> p a b", a=n_), r=[PS[b4]], w=[zraw])
            P.tt("dve", zd[:, :, :], zraw[:, :, 0:128], zraw[:, :, 1:129], ALU.subtract, r=[zraw], w=[zd])
            P.tt("dve", zd[:, :, :], zd[:, :, :], mub, ALU.mult, r=[zd, vec], w=[zd])
            P.tt("dve", zd[:, :, :], zd[:, :, :], zraw[:, :, 1:129], ALU.add, r=[zd, zraw], w=[zd])
            P.cp("pool", zraw[:, :, 0:1], zraw[:, :, 128:129], r=[zraw, zd], w=[zraw])
            if RL < 3:
                continue
            P.act(twd[:, :], zd[0:64, 12, :], AF.Tanh, r=[zd], w=[twd])
            P.act(sgd[:, :], zd[:, 13, :], AF.Sigmoid, r=[zd], w=[sgd])
            P.mm(PS[4][:, :], twd[:, :], wa2[0:64, :], start=True, stop=False, r=[twd, wa2], w=[PS[4]])
            P.mm(PS[4][:, :], onesf[0:1, 0:128], w0r[0:1, :], start=False, stop=True, r=[cf, w0r], w=[PS[4]])
            P.act(sgw[:, :], PS[4][:, :], AF.Sigmoid, r=[PS[4]], w=[sgw])
            P.mm(PS[5][:, :], sgd[:, :], g2t[:, :], r=[sgd, g2t], w=[PS[5]])
            P.cp("dve", g_tm[:, :], PS[5][:, :], r=[PS[5]], w=[g_tm])
            def hc_gen(hc, B):
                ex3, exn, aT, kkr, sq, rn, kk, tf, kp, bp, pr, AT, BT, KT, RT, KbT, BbT, tm, Nm, Lm, S1, NRB, X, RhT, Y0s, GT, Hs = (B[n] for n in NAMES)
                if RL < 4:
                    return
                rT = zd[:, hc, :]
                kTt = zd[:, 4 + hc, :]
                vT = zd[:, 8 + hc, :]
                P.mm(PS[7][:, 0:384], sgw[:, hc * 128:(hc + 1) * 128], tri3[:, :], r=[sgw, tri3], w=[PS[7]])
                P.act(ex3[:, :], PS[7][:, 0:384], AF.Exp, r=[PS[7]], w=[ex3])
                P.act(exn[:, :], PS[7][:, 0:128], AF.Exp, r=[PS[7]], w=[exn], scale=-1.0)
                P.mm(PS[6][:, 0:128], wa2[64:128, hc * 128:(hc + 1) * 128], zd[64:128, 12, :], r=[wa2, zd], w=[PS[6]])
                P.act(aT[:, :], PS[6][:, 0:128], AF.Sigmoid, r=[PS[6], vec], w=[aT], bias=vec[:, oa0 + hc:oa0 + hc + 1])
                yield
                P.ts("dve", kkr[:, :], kTt, vec[:, okk + hc:okk + hc + 1], ALU.mult, r=[zd, vec], w=[kkr])
                P.tt("dve", sq[:, :], kkr[:, :], kkr[:, :], ALU.mult, r=[kkr], w=[sq])
                P.mm(PS[6][:, 128:256], bd2[:, :], sq[:, :], r=[bd2, sq], w=[PS[6]])
                P.act(rn[:, :], PS[6][:, 128:256], AF.Sqrt, r=[PS[6]], w=[rn])
                P.ts("dve", rn[:, :], rn[:, :], 1e-12, ALU.max, r=[rn], w=[rn])
                P.recip(rn[:, :], rn[:, :], r=[rn], w=[rn])
                P.tt("dve", kk[:, :], kkr[:, :], rn[:, :], ALU.mult, r=[kkr, rn], w=[kk])
                yield
                P.ts("dve", tf[:, :], aT[:, :], vec[:, oka + hc:oka + hc + 1], ALU.mult, omka[:, hc:hc + 1], ALU.add,
                     r=[aT, vec, omka], w=[tf])
                P.tt("dve", kp[:, :], kTt, tf[:, :], ALU.mult, r=[zd, tf], w=[kp])
                P.stt(bp[:, :], aT[:, :], -1.0, kk[:, :], ALU.mult, ALU.mult, r=[aT, kk], w=[bp])
                yield
                P.tt("dve", RR(AT[:, :]), kk[:, :], ex3[:, 128:256], ALU.mult, r=[kk, ex3], w=[AT])
                P.tt("dve", RR(BT[:, :]), bp[:, :], exn[:, :], ALU.mult, r=[bp, exn], w=[BT])
                P.tt("dve", RR(KT[:, :]), kp[:, :], exn[:, :], ALU.mult, r=[kp, exn], w=[KT])
                P.tt("dve", RR(RT[:, :]), rT, ex3[:, 0:128], ALU.mult, r=[zd, ex3], w=[RT])
                P.tt("dve", KbT[:, :], kp[:, :], ex3[:, 256:384], ALU.mult, r=[kp, ex3], w=[KbT])
                P.tt("pool", BbT[:, :], bp[:, :], ex3[:, 256:384], ALU.mult, r=[bp, ex3], w=[BbT])
                yield
                P.stt(pr[:, :], rT, vec[:, ork + hc:ork + hc + 1], kp[:, :], ALU.mult, ALU.mult, r=[zd, vec, kp], w=[pr])
                P.mm(PS[5][:, 2 * hc:2 * hc + 2], pr[:, :], ind2[:, :], r=[pr, ind2], w=[PS[5]], skip=True)
                if RL < 5:
                    return
                yield
                for k_, src in enumerate((AT[:, :], KbT[:, :], BbT[:, :], vT)):
                    P.tr(PS[4][:, k_ * 128:(k_ + 1) * 128], src, identf, r=[AT, KbT, BbT, zd, cf], w=[PS[4]])
                P.cp("act", RR(tm[:, :, :]), PS[4][:, :].rearrange("p (a b) -> p a b", a=4), r=[PS[4]], w=[tm])
                P.cp("pool", vtm[:, hc * 128:(hc + 1) * 128], tm[:, 3, :], r=[tm], w=[(vtm, hc)])
                if RL < 6:
                    return
                yield "core"
                def pair_mm(dst, lhs, rhs, mask, eng="dve"):
                    for hh in range(2):
                        po = hh * 64
                        bank = nb()
                        P.mm(bank[:, 0:128], RR(lhs[po:po + 64, :]), RR(rhs[po:po + 64, :]), r=[lhs, rhs], w=[bank])
                        P.tt("dve", RR(dst[:, hh, :]), bank[:, 0:128], mask[:, :], ALU.mult, r=[bank, mask], w=[dst])
                pair_mm(Nm, BT, AT, ms)
                yield
                pair_mm(Lm, AT, BT, msl)
                yield
                pair_mm(S1, KT, AT, ms)
                yield
                pair_mm(NRB, BT, RT, mi)
                yield
                P.cp("dve", RR(X[:, :, 0:64]), tm[:, 0, :].rearrange("p (a b) -> p a b", a=2), r=[tm], w=[X])
                bank = nb()
                for hh in range(2):
                    po = hh * 64
                    P.mm(bank[:, hh * 64:(hh + 1) * 64], RR(S1[:, hh, :]), RR(tm[:, 3, po:po + 64]), r=[S1, tm], w=[bank], skip=True)
                P.cp("act", RR(X[:, :, 64:128]), bank[:, 0:128].rearrange("p (a b) -> p a b", a=2), r=[bank], w=[X])
                pair_mm(S1, KT, RT, mi)
                yield
                for lev in range(6):
                    bx = nb()
                    for hh in range(2):
                        P.mm(bx[:, hh * 128:(hh + 1) * 128], RR(Nm[:, hh, :]), RR(X[:, hh, :]), r=[Nm, X], w=[bx], skip=True)
                    if lev < 5:
                        bn = nb()
                        bl = nb()
                        for hh in range(2):
                            P.mm(bn[:, hh * 128:(hh + 1) * 128], RR(Lm[:, hh, :]), RR(Nm[:, hh, :]), r=[Nm, Lm], w=[bn], skip=True)
                        for hh in range(2):
                            P.mm(bl[:, hh * 128:(hh + 1) * 128], RR(Nm[:, hh, :]), RR(Lm[:, hh, :]), r=[Nm, Lm], w=[bl], skip=True)
                    P.tt("dve", RR(X[:, :, :]), X[:, :, :], bx[:, 0:256].rearrange("p (a b) -> p a b", a=2), ALU.add,
                         r=[X, bx], w=[X])
                    if lev < 5:
                        P.cp("act", RR(Nm[:, :, :]), bn[:, 0:256].rearrange("p (a b) -> p a b", a=2), r=[bn], w=[Nm])
                        P.cp("dve", RR(Lm[:, :, :]), bl[:, 0:256].rearrange("p (a b) -> p a b", a=2), r=[bl], w=[Lm])
                    yield
                bank = nb()
                for hh in range(2):
                    po = hh * 64
                    P.mm(bank[po:po + 64, 0:128], X[:, hh, 0:64], NRB[:, hh, :], r=[X, NRB], w=[bank], skip=True)
                P.tt("dve", RhT[:, :], RT[:, :], bank[:, 0:128], ALU.add, r=[RT, bank], w=[RhT])
                bank = nb()
                for hh in range(2):
                    po = hh * 64
                    P.mm(bank[:, hh * 64:(hh + 1) * 64], RR(S1[:, hh, :]), RR(tm[:, 3, po:po + 64]), start=True, stop=False,
                         r=[S1, tm], w=[bank], skip=True)
                    P.mm(bank[:, hh * 64:(hh + 1) * 64], RR(NRB[:, hh, :]), RR(X[:, hh, 64:128]), start=False, stop=True,
                         r=[NRB, X], w=[bank], skip=True)
                P.cp("act", Y0s[:, :], bank[:, 0:128], r=[bank], w=[Y0s])
                yield
                for c in range(2):
                    cs_ = slice(c * 64, (c + 1) * 64)
                    bg = nb()
                    bh = nb()
                    for hh in range(2):
                        po = hh * 64
                        P.mm(bg[po:po + 64, 0:64], X[cs_, hh, 0:64], tm[cs_, 2, po:po + 64],
                             r=[X, tm], w=[bg], skip=True)
                        P.mm(bh[po:po + 64, 0:64], tm[cs_, 1, po:po + 64], tm[cs_, 3, po:po + 64],
                             start=True, stop=False, r=[tm], w=[bh], skip=True)
                        P.mm(bh[po:po + 64, 0:64], tm[cs_, 2, po:po + 64], X[cs_, hh, 64:128],
                             start=False, stop=True, r=[tm, X], w=[bh], skip=True)
                    P.stt(GT[:, c, :], i2[:, :], ex3[:, 64 * c + 63:64 * c + 64], bg[:, 0:64],
                          ALU.mult, ALU.add, r=[i2, ex3, bg], w=[GT])
                    P.cp("act", Hs[:, c, :], bh[:, 0:64], r=[bh], w=[Hs])
                    yield
                for c in range(2):
                    cs_ = slice(c * 64, (c + 1) * 64)
                    for hh in range(2):
                        po = hh * 64
                        by = nb()
                        bm = nb()
                        P.mm(by[cs_, 0:64], RhT[po:po + 64, cs_], Mst[po:po + 64, hc, :],
                             r=[RhT, (Mst, hc)], w=[by])
                        P.mm(bm[po:po + 64, 0:64], GT[po:po + 64, c, :], Mst[po:po + 64, hc, :],
                             r=[GT, (Mst, hc)], w=[bm])
                        P.tt("dve", y_tm[cs_, hc * 128 + hh * 64:hc * 128 + (hh + 1) * 64], Y0s[cs_, hh * 64:(hh + 1) * 64],
                             by[cs_, 0:64], ALU.add, r=[Y0s, by], w=[(y_tm, hc)])
                        P.tt("dve", Mst[po:po + 64, hc, :], bm[po:po + 64, 0:64], Hs[po:po + 64, c, :], ALU.add,
                             r=[bm, Hs], w=[(Mst, hc)])
                    yield
            gens_ = [hc_gen(h_, LB[h_ % 2]) for h_ in range(4)]

            def to_core(g_):
                for v_ in g_:
                    if v_ == "core":
                        return True
                return False

            ok_ = to_core(gens_[0])
            for h_ in range(4):
                cur_ = gens_[h_] if ok_ else None
                nxt_ = gens_[h_ + 1] if h_ + 1 < 4 else None
                nxt_ready = False
                while cur_ is not None or (nxt_ is not None and not nxt_ready):
                    if cur_ is not None:
                        try:
                            next(cur_)
                        except StopIteration:
                            cur_ = None
                    if nxt_ is not None and not nxt_ready:
                        try:
                            if next(nxt_) == "core":
                                nxt_ready = True
                        except StopIteration:
                            nxt_ = None
                ok_ = nxt_ready
            if RL < 7:
                continue
            y3 = y_tm[:, :].rearrange("p (h d) -> p h d", h=8)
            P.red(st[:, 0, :], y3, ALU.add, r=[y_tm], w=[st])
            P.tt("dve", ysq[:, :], y_tm[:, :], y_tm[:, :], ALU.mult, r=[y_tm], w=[ysq])
            P.red(st[:, 1, :], ysq[:, :].rearrange("p (h d) -> p h d", h=8), ALU.add, r=[ysq], w=[st])
            P.ts("dve", st[:, 0, :], st[:, 0, :], 1.0 / 64, ALU.mult, r=[st], w=[st])
            P.tt("dve", st[:, 2, :], st[:, 0, :], st[:, 0, :], ALU.mult, r=[st], w=[st])
            P.stt(st[:, 1, :], st[:, 1, :], 1.0 / 64, st[:, 2, :], ALU.mult, ALU.subtract, r=[st], w=[st])
            P.act(st[:, 1, :], st[:, 1, :], AF.Sqrt, r=[st], w=[st], bias=64e-5)
            P.recip(st[:, 1, :], st[:, 1, :], r=[st], w=[st])
            P.tt("dve", ysq[:, :].rearrange("p (h d) -> p h d", h=8), y3,
                 st[:, 0, :].unsqueeze(2).to_broadcast([128, 8, 64]), ALU.subtract, r=[y_tm, st], w=[ysq])
            P.tt("dve", ysq[:, :].rearrange("p (h d) -> p h d", h=8), ysq[:, :].rearrange("p (h d) -> p h d", h=8),
                 st[:, 1, :].unsqueeze(2).to_broadcast([128, 8, 64]), ALU.mult, r=[ysq, st], w=[ysq])
            P.tt("dve", ysq[:, :], ysq[:, :], lnw[:, :], ALU.mult, r=[ysq, lnw], w=[ysq])
            P.tt("dve", ysq[:, :], ysq[:, :], lnb[:, :], ALU.add, r=[ysq, lnb], w=[ysq])
            P.cp("act", bon[:, :], PS[5][:, 0:8], r=[PS[5]], w=[bon])
            P.tt("dve", y_tm[:, :].rearrange("p (h d) -> p h d", h=8), vtm[:, :].rearrange("p (h d) -> p h d", h=8),
                 bon[:, :].unsqueeze(2).to_broadcast([128, 8, 64]), ALU.mult, r=[vtm, bon, ysq], w=[y_tm])
            P.tt("dve", ysq[:, :], ysq[:, :], y_tm[:, :], ALU.add, r=[ysq, y_tm], w=[ysq])
            P.tt("dve", ysq[:, :], ysq[:, :], g_tm[:, :], ALU.mult, r=[ysq, g_tm], w=[ysq])
            for k_ in range(4):
                P.tr(PS[6][:, k_ * 128:(k_ + 1) * 128], ysq[:, k_ * 128:(k_ + 1) * 128], identf, r=[ysq, cf], w=[PS[6]])
            P.cp("act", obT[:, :, :], PS[6][:, :].rearrange("p (a b) -> p a b", a=4), r=[PS[6]], w=[obT])
            for fc in range(8):
                bank = PS[fc // 4]
                for kc in range(4):
                    P.mm(bank[:, (fc % 4) * 128:(fc % 4 + 1) * 128], woB[:, kc, fc * 128:(fc + 1) * 128], obT[:, kc, :],
                         start=(kc == 0), stop=(kc == 3), r=[woB, obT], w=[bank])
            for fc in range(8):
                bank = PS[fc // 4]
                P.stt(xT[:, fc, tsl], bank[:, (fc % 4) * 128:(fc % 4 + 1) * 128], modT[:, 16 + fc:17 + fc], xT[:, fc, tsl],
                      ALU.mult, ALU.add, r=[bank, modT, (xT, i)], w=[(xT, i)])
        P.flush()


def nsa_phase(nc, P, l, env):
    sb, xT, hT, PS, PSB, loadc = (env[k] for k in ("sb", "xT", "hT", "PS", "PSB", "loadc"))
    identb, w_in, row_d, modT, vec, outproj_add = (env[k] for k in ("identb", "w_in", "row_d", "modT", "vec", "outproj_add"))
    nsa_w1, nsa_w2 = env["nsa_w1"], env["nsa_w2"]
    NC_ = 652
    C0 = 2500
    ope = VC["pe"][0]
    with ExitStack() as ph:
        ocT = sb(ph, "ocT", [128, 2, T], BF16, nsub=NT)
        with ExitStack() as pa:
            wC = sb(pa, "wC", [128, 8, NC_], BF16)
            ksT = sb(pa, "ksT", [64, T], BF16, nsub=NT)
            kwT = sb(pa, "kwT", [64, T], BF16, nsub=NT)
            Vs = sb(pa, "Vs", [128, NT, 65], BF16, nsub=NT)
            Vw = sb(pa, "Vw", [128, NT, 65], BF16, nsub=NT)
            kcmpT = sb(pa, "kcmpT", [64, 128], BF16)
            vcp = sb(pa, "vcp", [128, 97], BF16)
            gK = sb(pa, "gK", [128, 64])
            gQ = sb(pa, "gQ", [128, 256])
            cs = loadc(pa, "cs")
            cs_default = cs
            addm = loadc(pa, "addm")
            efb = sb(pa, "efb", [128, T], BF16)
            validb = sb(pa, "validb", [128, T], BF16)
            negu4 = sb(pa, "negu4b", [128, 512], BF16)
            negl4 = sb(pa, "negl4b", [128, 512], BF16)
            with ExitStack() as pb:
                wst = sb(pb, "wCst", [128, 4, NC_])
                for hf in range(2):
                    P.dma(wst[:, :, :], w_in[l, hf * 512:(hf + 1) * 512, C0:C0 + NC_].rearrange("(k p) n -> p k n", p=128),
                          r=[wst], w=[wst])
                    P.cp("pool" if hf else "dve", wC[:, hf * 4:(hf + 1) * 4, :], wst[:, :, :], r=[wst], w=[wC])
                for dst_, name_, rows_ in ((efb, "efull", 32), (validb, "validT", 127), (negu4, "negu4", 128),
                                           (negl4, "negl4", 128)):
                    o_, wd_ = CF[name_]
                    st_ = sb(pb, "cst_" + name_, [128, wd_])
                    P.dma(st_[0:rows_, :], env["cf_d"][0:rows_, o_:o_ + wd_], w=[st_])
                    P.cp("pool", dst_[0:rows_, :], st_[0:rows_, :], r=[st_], w=[dst_])
                P.flush()
            o, w_ = RW["nsa_kg"]
            P.dma(gK[:, :], row_d[l:l + 1, o:o + w_].partition_broadcast(128), w=[gK])
            o, w_ = RW["nsa_qg"]
            P.dma(gQ[:, :], row_d[l:l + 1, o:o + w_].partition_broadcast(128), w=[gQ])
            P.memset("pool", Vs[:, :, :], 1.0, w=[Vs])
            P.memset("pool", Vw[:, :, :], 1.0, w=[Vw])
            P.memset("pool", vcp[:, :], 0.0, w=[vcp])

            def norm_rope(src, ng, gain, cosb, sinb, dst, tmp, rows=128, tab=None):
                sqv, ssv, ta, tb_, tc, td = tmp
                cs = tab if tab is not None else cs_default
                R = slice(0, rows)
                P.tt("dve", sqv[R, 0:ng, :], src, src, ALU.mult, r=[src_b], w=[sqv])
                P.red(ssv[R, 0:ng], sqv[R, 0:ng, :], ALU.add, r=[sqv], w=[ssv])
                P.act(ssv[R, 0:ng], ssv[R, 0:ng], AF.Sqrt, r=[ssv], w=[ssv], bias=1e-6, scale=1.0 / 64)
                P.recip(ssv[R, 0:ng], ssv[R, 0:ng], r=[ssv], w=[ssv])
                P.tt("dve", src, src, ssv[R, 0:ng].unsqueeze(2).to_broadcast([rows, ng, 64]), ALU.mult, r=[src_b, ssv], w=[src_b])
                P.tt("pool", src, src, gain, ALU.mult, r=[src_b, gK, gQ], w=[src_b])
                P.tt("dve", ta[R, 0:ng, :], src[:, :, 0:32], cosb, ALU.mult, r=[src_b, cs], w=[ta])
                P.tt("pool", tb_[R, 0:ng, :], src[:, :, 32:64], sinb, ALU.mult, r=[src_b, cs], w=[tb_])
                P.tt("pool", tc[R, 0:ng, :], src[:, :, 32:64], cosb, ALU.mult, r=[src_b, cs], w=[tc])
                P.tt("dve", td[R, 0:ng, :], src[:, :, 0:32], sinb, ALU.mult, r=[src_b, cs], w=[td])
                P.tt("dve", dst[:, :, 0:32], ta[R, 0:ng, :], tb_[R, 0:ng, :], ALU.subtract, r=[ta, tb_], w=[dst_b])
                P.tt("pool", dst[:, :, 32:64], tc[R, 0:ng, :], td[R, 0:ng, :], ALU.add, r=[tc, td], w=[dst_b])

            tmp = (sb(pa, "nsq", [128, 4, 64]), sb(pa, "nss", [128, 4]), sb(pa, "nta", [128, 4, 32]),
                   sb(pa, "ntb", [128, 4, 32]), sb(pa, "ntc", [128, 4, 32]), sb(pa, "ntd", [128, 4, 32]))
            rwq = sb(pa, "rwq", [128, 4, 64])
            roq = sb(pa, "roq", [128, 4, 64], BF16)
            with ExitStack() as pb:
                kcT = sb(pb, "kcT", [64, T], BF16, nsub=NT)
                vcT = sb(pb, "vcT", [64, T], BF16, nsub=NT)
                zs = sb(pb, "zsn", [128, 384])
                kvb = sb(pb, "kvb", [128, 2, 64], BF16)
                for i in range(NT):
                    tsl = slice(i * 128, (i + 1) * 128)
                    for kc in range(8):
                        P.mm(PS[0][:, 0:384], hT[:, kc, tsl], wC[:, kc, 256:640], start=(kc == 0), stop=(kc == 7),
                             r=[(hT, i), wC], w=[PS[0]])
                    P.cp("act", zs[:, :], PS[0][:, 0:384], r=[PS[0]], w=[zs])
                    P.cp("pool", kvb[:, :, :], zs[:, 0:128].rearrange("p (g d) -> p g d", g=2), r=[zs], w=[kvb])
                    P.cp("act", Vs[:, i, 0:64], zs[:, 192:256], r=[zs], w=[(Vs, i)])
                    P.cp("act", Vw[:, i, 0:64], zs[:, 320:384], r=[zs], w=[(Vw, i)])
                    P.cp("dve", rwq[:, 0, :], zs[:, 128:192], r=[zs], w=[rwq])
                    P.cp("dve", rwq[:, 1, :], zs[:, 256:320], r=[zs], w=[rwq])
                    src_b, dst_b = rwq, roq
                    norm_rope(rwq[:, 0:2, :], 2, gK[:, :].unsqueeze(1).to_broadcast([128, 2, 64]),
                              cs[:, i * 64:i * 64 + 32].unsqueeze(1).to_broadcast([128, 2, 32]),
                              cs[:, i * 64 + 32:i * 64 + 64].unsqueeze(1).to_broadcast([128, 2, 32]), roq[:, 0:2, :], tmp)
                    P.tr(PSB[1][0:64, 0:128], roq[:, 0, :], identb[:, :], r=[roq, identb], w=[PS[1]])
                    P.tr(PSB[1][0:64, 128:256], roq[:, 1, :], identb[:, :], r=[roq, identb], w=[PS[1]])
                    P.tr(PSB[1][0:64, 256:384], kvb[:, 0, :], identb[:, :], r=[kvb, identb], w=[PS[1]])
                    P.tr(PSB[1][0:64, 384:512], kvb[:, 1, :], identb[:, :], r=[kvb, identb], w=[PS[1]])
                    P.cp("dve", ksT[:, tsl], PSB[1][0:64, 0:128], r=[PS[1]], w=[(ksT, i)])
                    P.cp("dve", kwT[:, tsl], PSB[1][0:64, 128:256], r=[PS[1]], w=[(kwT, i)])
                    P.cp("dve", kcT[:, tsl], PSB[1][0:64, 256:384], r=[PS[1]], w=[(kcT, i)])
                    P.cp("dve", vcT[:, tsl], PSB[1][0:64, 384:512], r=[PS[1]], w=[(vcT, i)])
                blk = sb(pb, "blk", [64, 32, 127], BF16)
                w1s = [sb(pb, f"w1s{q}", [64, 8, 256]) for q in range(2)]
                w1b = [sb(pb, f"w1b{q}", [64, 8, 256], BF16) for q in range(2)]
                w2s = sb(pb, "w2s", [128, 2, 64])
                w2b = sb(pb, "w2b", [128, 2, 64], BF16)
                GT_ = sb(pb, "GTc", [128, 2, 127], BF16)
                gx2 = sb(pb, "gx2", [128, 127])
                gu = sb(pb, "gu", [128, 127])
                cmo = sb(pb, "cmo", [128, 1, 64])
                cmb = sb(pb, "cmb", [128, 1, 64], BF16)
                csc = loadc(pb, "cs_cmp")
                ovl = loadc(pb, "ovl")
                cnt = 0
                for j in range(2):
                    srcT = kcT if j == 0 else vcT
                    lo = srcT[:, 0:T].rearrange("d (n s) -> d n s", s=16)
                    hi = srcT[:, 16:T].rearrange("d (n s) -> d n s", s=16)
                    for l_ in range(32):
                        sv = lo[:, 0:127, l_] if l_ < 16 else hi[:, 0:127, l_ - 16]
                        P.ts("dve" if l_ % 2 else "pool", blk[:, l_, :], sv,
                             vec[0:64, ope + j * 32 + l_:ope + j * 32 + l_ + 1], ALU.add, r=[srcT, vec], w=[blk])
                    P.dma(w2s[:, :, :], nsa_w2[l, j].rearrange("(c p) d -> p c d", p=128), r=[w2s], w=[w2s])
                    P.cp("dve", w2b[:, :, :], w2s[:, :, :], r=[w2s], w=[w2b])
                    for lg in range(4):
                        ws, wb = w1s[cnt % 2], w1b[cnt % 2]
                        cnt += 1
                        P.dma(ws[:, :, :], nsa_w1[l, j, lg * 512:(lg + 1) * 512, :].rearrange("(l d) c -> d l c", d=64),
                              r=[ws], w=[ws])
                        P.cp("pool" if lg % 2 else "dve", wb[:, :, :], ws[:, :, :], r=[ws], w=[wb])
                        for lq in range(8):
                            l_ = lg * 8 + lq
                            for cc in range(2):
                                P.mm(PS[2 + cc][:, 0:127], wb[:, lq, cc * 128:(cc + 1) * 128], blk[:, l_, :],
                                     start=(l_ == 0), stop=(l_ == 31), r=[wb, blk], w=[PS[2 + cc]])
                    for cc in range(2):
                        pz = PS[2 + cc][:, 0:127]
                        P.act(gx2[:, :], pz, AF.Square, r=[PS[2 + cc]], w=[gx2])
                        P.ts("dve", gx2[:, :], gx2[:, :], 0.044715, ALU.mult, 1.0, ALU.add, r=[gx2], w=[gx2])
                        P.tt("dve", gu[:, :], gx2[:, :], pz, ALU.mult, r=[gx2, PS[2 + cc]], w=[gu])
                        P.act(gu[:, :], gu[:, :], AF.Sigmoid, r=[gu], w=[gu], scale=1.5957691216057308)
                        P.tt("dve", GT_[:, cc, :], gu[:, :], pz, ALU.mult, r=[gu, PS[2 + cc]], w=[GT_])
                    for cc in range(2):
                        P.mm(PS[4][0:127, 0:64], GT_[:, cc, :], w2b[:, cc, :], start=(cc == 0), stop=(cc == 1),
                             r=[GT_, w2b], w=[PS[4]])
                    if j == 0:
                        P.cp("act", cmo[0:127, 0, :], PS[4][0:127, 0:64], r=[PS[4]], w=[cmo])
                        src_b, dst_b = cmo, cmb
                        norm_rope(cmo[0:127, 0:1, :], 1, gK[0:127, :].unsqueeze(1),
                                  csc[0:127, 0:32].unsqueeze(1), csc[0:127, 32:64].unsqueeze(1), cmb[0:127, 0:1, :], tmp, rows=127, tab=csc)
                        P.tr(PSB[5][0:64, 0:127], cmb[0:127, 0, :], identb[0:127, 0:127], r=[cmb, identb], w=[PS[5]])
                        P.cp("dve", kcmpT[:, 0:127], PSB[5][0:64, 0:127], r=[PS[5]], w=[kcmpT])
                    else:
                        P.cp("act", vcp[0:127, 0:64], PS[4][0:127, 0:64], r=[PS[4]], w=[vcp])
                        P.cp("dve", vcp[0:127, 64:97], ovl[0:127, :], r=[ovl], w=[vcp])
                P.flush()
            qT = sb(pa, "qTn", [64, 2, 512], BF16, nsub=2)
            gt = sb(pa, "gt", [128, 12])
            pc = sb(pa, "pc", [128, 512], BF16)
            pT = sb(pa, "pTn", [128, 2, 512], BF16, nsub=2)
            oc = sb(pa, "oc", [128, 4, 64])
            ocb = sb(pa, "ocb", [128, 256], BF16)
            imp = sb(pa, "imp", [128, 32])
            imw = sb(pa, "imw", [128, 32])
            m8a = sb(pa, "m8a", [128, 8])
            m8b = sb(pa, "m8b", [128, 8])
            nmb = sb(pa, "nmb", [128, 32], BF16)
            nmT4 = sb(pa, "nmT4", [32, 4, 128], BF16)
            rec = sb(pa, "recn", [128, 4])
            coef = sb(pa, "coef", [128, 4])

            def finish(gcol, first):
                for h in range(4):
                    acc = PS[4 + h]
                    if first:
                        P.ts("dve", rec[:, h:h + 1], acc[:, 64:65], 1e-30, ALU.max, r=[acc], w=[rec])
                        P.recip(rec[:, h:h + 1], rec[:, h:h + 1], r=[rec], w=[rec])
                    else:
                        P.recip(rec[:, h:h + 1], acc[:, 64:65], r=[acc], w=[rec])
                    P.tt("dve", coef[:, h:h + 1], rec[:, h:h + 1], gt[:, 3 * h + gcol:3 * h + gcol + 1], ALU.mult,
                         r=[rec, gt], w=[coef])
                    if first:
                        P.ts("dve", oc[:, h, :], acc[:, 0:64], coef[:, h:h + 1], ALU.mult, r=[acc, coef], w=[oc])
                    else:
                        P.stt(oc[:, h, :], acc[:, 0:64], coef[:, h:h + 1], oc[:, h, :], ALU.mult, ALU.add,
                              r=[acc, coef, oc], w=[oc])

            for i in range(NT):
                tsl = slice(i * 128, (i + 1) * 128)
                b = i % 2
                for kc in range(8):
                    P.mm(PS[0][:, 0:256], hT[:, kc, tsl], wC[:, kc, 0:256], start=(kc == 0), stop=(kc == 7),
                         r=[(hT, i), wC], w=[PS[0]])
                for kc in range(8):
                    P.mm(PS[1][:, 0:12], hT[:, kc, tsl], wC[:, kc, 640:652], start=(kc == 0), stop=(kc == 7),
                         r=[(hT, i), wC], w=[PS[1]])
                P.cp("act", rwq[:, :, :], PS[0][:, 0:256].rearrange("p (g d) -> p g d", g=4), r=[PS[0]], w=[rwq])
                P.act(gt[:, :], PS[1][:, 0:12], AF.Sigmoid, r=[PS[1]], w=[gt])
                src_b, dst_b = rwq, roq
                norm_rope(rwq[:, :, :], 4, gQ[:, :].rearrange("p (g d) -> p g d", g=4),
                          cs[:, i * 64:i * 64 + 32].unsqueeze(1).to_broadcast([128, 4, 32]),
                          cs[:, i * 64 + 32:i * 64 + 64].unsqueeze(1).to_broadcast([128, 4, 32]), roq[:, :, :], tmp)
                for g in range(4):
                    P.tr(PSB[0][0:64, 512 + g * 128:512 + (g + 1) * 128], roq[:, g, :], identb[:, :], r=[roq, identb], w=[PS[0]])
                P.cp("dve", qT[:, b, :], PSB[0][0:64, 512:1024], r=[PS[0]], w=[(qT, b)])
                nv = 8 * i + 7
                P.mm(PS[2][0:nv, :], kcmpT[:, 0:nv], qT[:, b, :], r=[kcmpT, (qT, b)], w=[PS[2]])
                P.act(pc[0:nv, :], PS[2][0:nv, :], AF.Exp, r=[PS[2]], w=[pc], scale=0.125)
                P.tt("dve", pc[0:nv, :].rearrange("p (h t) -> p h t", h=4), pc[0:nv, :].rearrange("p (h t) -> p h t", h=4),
                     validb[0:nv, tsl].unsqueeze(1).to_broadcast([nv, 4, 128]), ALU.mult, r=[pc, validb], w=[pc])
                for h in range(4):
                    P.mm(PS[4 + h][:, 0:97], pc[0:nv, h * 128:(h + 1) * 128], vcp[0:nv, :], r=[pc, vcp], w=[PS[4 + h]])
                finish(0, True)
                for h in range(4):
                    acc = PS[4 + h]
                    if h == 0:
                        P.ts("dve", imp[:, :], acc[:, 65:97], rec[:, 0:1], ALU.mult, r=[acc, rec], w=[imp])
                    else:
                        P.stt(imp[:, :], acc[:, 65:97], rec[:, h:h + 1], imp[:, :], ALU.mult, ALU.add,
                              r=[acc, rec, imp], w=[imp])
                P.tt("dve", imp[:, :], imp[:, :], addm[:, i * 32:(i + 1) * 32], ALU.add, r=[imp, addm], w=[imp])
                P.op("dve", lambda e: e.max(out=m8a[:, :], in_=imp[:, :]), r=[imp], w=[m8a])
                P.op("dve", lambda e: e.match_replace(out=imw[:, :], in_to_replace=m8a[:, :], in_values=imp[:, :],
                                                      imm_value=-3.0e38), r=[imp, m8a], w=[imw])
                P.op("dve", lambda e: e.max(out=m8b[:, :], in_=imw[:, :]), r=[imw], w=[m8b])
                P.ts("dve", nmb[:, :], imp[:, :], m8b[:, 7:8], ALU.is_lt, -BIG, ALU.mult, r=[imp, m8b], w=[nmb])
                P.tr(PSB[1][0:32, 0:128], nmb[:, :], identb[:, :], r=[nmb, identb], w=[PS[1]])
                P.cp("dve", nmT4[:, :, :], PSB[1][0:32, 0:128].unsqueeze(1).to_broadcast([32, 4, 128]), r=[PS[1]], w=[nmT4])
                for j in range(i + 1):
                    bank = PS[2 + j % 2]
                    jsl = slice(j * 128, (j + 1) * 128)
                    P.mm(bank[:, :], ksT[:, jsl], qT[:, b, :], start=True, stop=False, r=[(ksT, j), (qT, b)], w=[bank])
                    P.mm(bank[:, :], efb[0:32, jsl], nmT4[:, :, :].rearrange("p h t -> p (h t)"), start=False, stop=(j != i),
                         r=[efb, nmT4], w=[bank])
                    if j == i:
                        P.mm(bank[:, :], identb[:, :], negu4[:, :], start=False, stop=True, r=[identb, negu4], w=[bank])
                    P.act(pT[:, j % 2, :], bank[:, :], AF.Exp, r=[bank], w=[(pT, j % 2)], scale=0.125)
                    for h in range(4):
                        P.mm(PS[4 + h][:, 0:65], pT[:, j % 2, h * 128:(h + 1) * 128], Vs[:, j, :],
                             start=(j == 0), stop=(j == i), r=[(pT, j % 2), (Vs, j)], w=[PS[4 + h]])
                finish(1, False)
                j0 = max(0, i - 4)
                for j in range(j0, i + 1):
                    bank = PS[2 + j % 2]
                    jsl = slice(j * 128, (j + 1) * 128)
                    extra = (j == i) or (j == i - 4)
                    P.mm(bank[:, :], kwT[:, jsl], qT[:, b, :], start=True, stop=not extra, r=[(kwT, j), (qT, b)], w=[bank])
                    if j == i:
                        P.mm(bank[:, :], identb[:, :], negu4[:, :], start=False, stop=True, r=[identb, negu4], w=[bank])
                    elif j == i - 4:
                        P.mm(bank[:, :], identb[:, :], negl4[:, :], start=False, stop=True, r=[identb, negl4], w=[bank])
                    P.act(pT[:, j % 2, :], bank[:, :], AF.Exp, r=[bank], w=[(pT, j % 2)], scale=0.125)
                    for h in range(4):
                        P.mm(PS[4 + h][:, 0:65], pT[:, j % 2, h * 128:(h + 1) * 128], Vw[:, j, :],
                             start=(j == j0), stop=(j == i), r=[(pT, j % 2), (Vw, j)], w=[PS[4 + h]])
                finish(2, False)
                P.cp("act", ocb[:, :], oc[:, :, :].rearrange("p h d -> p (h d)"), r=[oc], w=[ocb])
                for c in range(2):
                    P.tr(PSB[1][:, 512 + c * 128:512 + (c + 1) * 128], ocb[:, c * 128:(c + 1) * 128], identb[:, :],
                         r=[ocb, identb], w=[PS[1]])
                P.cp("dve", ocT[:, :, tsl], PSB[1][:, 512:768].rearrange("p (c t) -> p c t", c=2), r=[PS[1]], w=[(ocT, i)])
            P.flush()
        outproj_add(ocT, 2, 768, 16, l, "C")


def _pack_inputs(inp):
    L = DEPTH
    vecs = np.zeros((L, 128, NVC), np.float32)
    rows = np.zeros((L, NRW), np.float32)
    for l in range(L):
        def put(name, arr):
            o, w = VC[name]
            vecs[l, :, o:o + w] = arr
        put("ada_b", _col(inp["ada_b"][l]))
        put("n1g", _col(inp["norm1_g"][l]))
        put("n2g", _col(inp["norm2_g"][l]))
        put("mu", _col(inp["rwkv_mu"][l]))
        put("a0", _col(inp["rwkv_a0"][l]))
        put("k_k", _col(inp["rwkv_k_k"][l]))
        put("k_a", _col(inp["rwkv_k_a"][l]))
        put("r_k", _col(np.asarray(inp["rwkv_r_k"][l]).reshape(-1)))
        pe = np.asarray(inp["nsa_pe"][l], np.float32)
        o, w = VC["pe"]
        vecs[l, 0:64, o:o + 64] = pe.transpose(2, 0, 1).reshape(64, 64)

        def prow(name, arr):
            o, w = RW[name]
            rows[l, o:o + w] = np.asarray(arr, np.float32).reshape(-1)
        prow("dsa_g", np.concatenate([np.tile(inp["dsa_q_g"][l], 4), inp["dsa_k_g"][l]]))
        prow("nsa_qg", np.tile(inp["nsa_q_g"][l], 4))
        prow("nsa_kg", inp["nsa_k_g"][l])
        prow("ln_w", inp["rwkv_ln_w"][l])
        prow("ln_b", inp["rwkv_ln_b"][l])
        prow("w0", inp["rwkv_w0"][l])
    return vecs, rows


_SHARED = ("ada_w", "w_in", "w_out", "ffn_wi", "ffn_wo", "rwkv_w2", "rwkv_a2", "rwkv_g2", "nsa_w1", "nsa_w2")


def make_in_maps(inp):
    vecs, rows = _pack_inputs(inp)
    shared = {k: np.ascontiguousarray(np.asarray(inp[k], np.float32)) for k in _SHARED}
    maps = []
    for b in range(8):
        m = dict(shared)
        m["x"] = np.ascontiguousarray(np.asarray(inp["x"][b], np.float32))
        m["ccol"] = _col(inp["c"][b])
        m["cfd"] = CONSTS
        m["vec"] = vecs
        m["row"] = rows
        maps.append(m)
    return maps


def kernel(**inputs):
    nc = build()
    maps = make_in_maps(inputs)
    res = run_bass_kernel_spmd(nc, maps, core_ids=list(range(8)))
    return np.stack([np.asarray(r["y"], np.float32) for r in res.results], axis=0)
```

```python
import math
from contextlib import ExitStack

import numpy as np
import concourse.bass as bass
import concourse.mybir as mybir
from concourse.bass_utils import run_bass_kernel_spmd

F32 = mybir.dt.float32
BF16 = mybir.dt.bfloat16
F32R = mybir.dt.float32r


def RR(ap):
    return ap.bitcast(F32R)
AF = mybir.ActivationFunctionType
ALU = mybir.AluOpType
AX = mybir.AxisListType

D = 1024
T = 2048
NT = 16
DEPTH = 2
FFN_H = 2816
BIG = 32768.0
NEGF = -1.0e30


class Buf:
    def __init__(self, name, t, nsub=1):
        self.name = name
        self.t = t
        self.nsub = nsub

    def __getitem__(self, idx):
        return self.t[idx]


def keys_of(items):
    out = []
    for it in items:
        if isinstance(it, Buf):
            out.extend((it.name, s) for s in range(it.nsub))
        elif isinstance(it, tuple) and isinstance(it[0], Buf):
            b, s = it
            if isinstance(s, (list, tuple, range)):
                out.extend((b.name, x) for x in s)
            else:
                out.append((b.name, s))
        else:
            out.append(it)
    return out


class Op:
    __slots__ = ("eng", "fn", "reads", "writes", "dma", "idx", "signal", "sem", "val", "waits", "semkey")

    def __init__(self, eng, fn, reads, writes, dma):
        self.eng = eng
        self.fn = fn
        self.reads = reads
        self.writes = writes
        self.dma = dma
        self.signal = False
        self.sem = None
        self.val = 0
        self.waits = []
        self.semkey = None


class Prog:
    ENGS = ("pe", "act", "dve", "pool", "sp")

    def __init__(self, nc, sems, dma_sems):
        self.nc = nc
        self.sems = sems
        self.dma_sems = dma_sems
        self.ops = []
        self.cnt = {e: 0 for e in self.ENGS}
        self.dma_cum = [0] * len(dma_sems)
        self.dma_used = [False] * len(dma_sems)
        self.dma_rr = 0
        self.known = {e: {} for e in self.ENGS}
        self.n_ins = 0
        self.n_wait = 0

    def op(self, eng, fn, r=(), w=(), dma=False):
        o = Op(eng, fn, tuple(keys_of(r)), tuple(keys_of(w)), dma)
        o.idx = len(self.ops)
        self.ops.append(o)
        return o

    def flush(self):
        ops = self.ops
        self.ops = []
        if not ops:
            return
        fence = Op("sp", None, tuple(k for o in ops if o.dma for k in o.writes), (), False)
        fence.idx = len(ops)
        ops.append(fence)
        last_w = {}
        readers = {}
        deps_of = []
        for o in ops:
            deps = set()
            for k in o.reads:
                w = last_w.get(k)
                if w is not None:
                    deps.add(w)
            for k in o.writes:
                w = last_w.get(k)
                if w is not None:
                    deps.add(w)
                deps.update(readers.get(k, ()))
            deps.discard(o.idx)
            fd = []
            for d in deps:
                p = ops[d]
                if p.eng == o.eng and not p.dma:
                    if o.eng in ("pe", "sp"):
                        continue
                    if not (set(p.writes) & (set(o.reads) | set(o.writes))):
                        continue
                fd.append(d)
            deps_of.append(fd)
            for k in o.reads:
                readers.setdefault(k, []).append(o.idx)
            for k in o.writes:
                last_w[k] = o.idx
                readers[k] = []
        for fd in deps_of:
            for d in fd:
                ops[d].signal = True
        extra = {}
        nd = len(self.dma_sems)
        for o in ops:
            if o.dma:
                j = self.dma_rr % nd
                self.dma_rr += 1
                if self.dma_used[j]:
                    extra[o.idx] = (self.dma_sems[j], self.dma_cum[j], ("dma", j))
                self.dma_used[j] = True
                self.dma_cum[j] += 16
                o.sem = self.dma_sems[j]
                o.val = self.dma_cum[j]
                o.semkey = ("dma", j)
                o.signal = True
            elif o.signal:
                self.cnt[o.eng] += 1
                o.sem = self.sems[o.eng]
                o.val = self.cnt[o.eng]
                o.semkey = ("eng", o.eng)
        for o, fd in zip(ops, deps_of):
            need = {}
            for d in fd:
                p = ops[d]
                if need.get(p.semkey, (None, -1))[1] < p.val:
                    need[p.semkey] = (p.sem, p.val)
            if o.idx in extra:
                s, v, k = extra[o.idx]
                if need.get(k, (None, -1))[1] < v:
                    need[k] = (s, v)
            kn = self.known[o.eng]
            for k, (s, v) in need.items():
                if kn.get(k, -1) >= v:
                    continue
                kn[k] = v
                o.waits.append((s, v))
        by = {e: [o for o in ops if o.eng == e] for e in self.ENGS}
        self.n_ins += len(ops)
        self.n_wait += sum(len(o.waits) for o in ops)

        def run(engobj, lst):
            for o in lst:
                for s, v in o.waits:
                    engobj.wait_ge(s, v)
                if o.fn is None:
                    continue
                ins = o.fn(engobj)
                if o.signal:
                    ins.then_inc(o.sem, 16 if o.dma else 1)

        with self.nc.Block() as block:
            @block.tensor
            def _(e):
                run(e, by["pe"])

            @block.scalar
            def _(e):
                run(e, by["act"])

            @block.vector
            def _(e):
                run(e, by["dve"])

            @block.gpsimd
            def _(e):
                run(e, by["pool"])

            @block.sync
            def _(e):
                run(e, by["sp"])

    def dma(self, out, in_, r=(), w=(), q="sp"):
        return self.op(q, lambda e: e.dma_start(out=out, in_=in_), r, w, dma=True)

    def mm(self, out, lhsT, rhs, start=True, stop=True, r=(), w=(), skip=False):
        return self.op("pe", lambda e: e.matmul(out, lhsT=lhsT, rhs=rhs, start=start, stop=stop,
                                                skip_group_check=skip), r, w)

    def tr(self, out, in_, ident, r=(), w=()):
        return self.op("pe", lambda e: e.transpose(out, in_, ident), r, w)

    def act(self, out, in_, func, r=(), w=(), bias=None, scale=None):
        kw = {}
        if bias is not None:
            kw["bias"] = bias
        if scale is not None:
            kw["scale"] = scale
        return self.op("act", lambda e: e.activation(out=out, in_=in_, func=func, **kw), r, w)

    def cp(self, eng, out, in_, r=(), w=()):
        if eng == "act":
            return self.op("act", lambda e: e.copy(out=out, in_=in_), r, w)
        return self.op(eng, lambda e: e.tensor_copy(out=out, in_=in_), r, w)

    def tt(self, eng, out, in0, in1, op, r=(), w=()):
        return self.op(eng, lambda e: e.tensor_tensor(out=out, in0=in0, in1=in1, op=op), r, w)

    def ts(self, eng, out, in0, s1, op0, s2=None, op1=None, r=(), w=()):
        if op1 is None:
            return self.op(eng, lambda e: e.tensor_scalar(out=out, in0=in0, scalar1=s1, scalar2=None, op0=op0), r, w)
        return self.op(eng, lambda e: e.tensor_scalar(out=out, in0=in0, scalar1=s1, scalar2=s2, op0=op0, op1=op1), r, w)

    def stt(self, out, in0, scalar, in1, op0, op1, r=(), w=()):
        return self.op("dve", lambda e: e.scalar_tensor_tensor(out=out, in0=in0, scalar=scalar, in1=in1,
                                                               op0=op0, op1=op1), r, w)

    def red(self, out, in_, op, r=(), w=()):
        return self.op("dve", lambda e: e.tensor_reduce(out=out, in_=in_, axis=AX.X, op=op), r, w)

    def recip(self, out, in_, r=(), w=()):
        return self.op("dve", lambda e: e.reciprocal(out=out, in_=in_), r, w)

    def memset(self, eng, ap, val, w=()):
        return self.op(eng, lambda e: e.memset(ap, val), (), w)


def _rope_table(pos):
    half = 32
    inv = (np.float32(10000.0) ** (-np.arange(half, dtype=np.float32) / np.float32(half))).astype(np.float32)
    ang = pos.astype(np.float32)[:, None] * inv[None, :]
    return np.concatenate([np.cos(ang), np.sin(ang)], axis=1).astype(np.float32)


CF = {}


def _build_consts():
    parts = []
    off = 0

    def add(name, arr):
        nonlocal off
        a = np.zeros((128, arr.shape[1]), np.float32)
        a[: arr.shape[0]] = arr
        CF[name] = (off, arr.shape[1])
        parts.append(a)
        off += arr.shape[1]

    p = np.arange(128)
    add("ident", np.eye(128, dtype=np.float32))
    add("ones", np.ones((128, 128), np.float32))
    tab = _rope_table(np.arange(T))
    add("cs", tab.reshape(NT, 128, 64).transpose(1, 0, 2).reshape(128, NT * 64))
    up = (p[None, :] > p[:, None]).astype(np.float32)
    add("cneg", up * NEGF)
    add("cnegb", up * (-BIG))
    negu = (p[:, None] > p[None, :]).astype(np.float32) * (-BIG)
    negl = (p[:, None] <= p[None, :]).astype(np.float32) * (-BIG)
    add("negu4", np.tile(negu, (1, 4)))
    add("negl4", np.tile(negl, (1, 4)))
    add("i4", np.tile(np.eye(128, dtype=np.float32), (1, 4)))
    n = np.arange(127)
    add("cs_cmp", _rope_table(16 * n + 31))
    e = (np.arange(T)[None, :] // 64 == np.arange(32)[:, None]).astype(np.float32)
    add("efull", e)
    valid = (16 * n[:, None] + 31 <= np.arange(T)[None, :]).astype(np.float32)
    add("validT", valid)
    t = np.arange(T)
    blk = np.arange(32)
    cur = t // 64
    adm = blk[None, :] * 64 <= t[:, None]
    forced = (blk[None, :] == 0) | (blk[None, :] == cur[:, None]) | (blk[None, :] == cur[:, None] - 1)
    am = np.where(adm, np.where(forced, 1.0e30, 0.0), -1.0e30).astype(np.float32)
    add("addm", am.reshape(NT, 128, 32).transpose(1, 0, 2).reshape(128, NT * 32))
    starts = n * 16
    endp = starts + 31
    sel_start = blk * 64
    ov = ((starts[:, None] <= sel_start[None, :] + 63) & (endp[:, None] >= sel_start[None, :])).astype(np.float32)
    add("ovl", np.concatenate([np.ones((127, 1), np.float32), ov], axis=1))
    same = (p[:, None] // 64) == (p[None, :] // 64)
    incl = ((p[:, None] <= p[None, :]) & same).astype(np.float32)
    excl = ((p[:, None] < p[None, :]) & same).astype(np.float32)
    rev = ((p[:, None] > p[None, :]) & same).astype(np.float32)
    c = -math.exp(-0.5)
    add("tri3", np.concatenate([incl, excl, rev], axis=1) * np.float32(c))
    add("ms", excl)
    add("msl", rev)
    add("mi", incl)
    add("bd2", same.astype(np.float32))
    ind2 = np.zeros((128, 2), np.float32)
    ind2[:64, 0] = 1
    ind2[64:, 1] = 1
    add("ind2", ind2)
    add("i2", np.concatenate([np.eye(64, dtype=np.float32)] * 2, axis=0))
    return np.concatenate(parts, axis=1)


CONSTS = _build_consts()
NCF = CONSTS.shape[1]

VC = {"ada_b": (0, 48), "n1g": (48, 8), "n2g": (56, 8), "mu": (64, 14), "a0": (78, 4), "k_k": (82, 4),
      "k_a": (86, 4), "r_k": (90, 4), "pe": (94, 64)}
NVC = 158
RW = {"dsa_g": (0, 320), "nsa_qg": (320, 256), "nsa_kg": (576, 64), "ln_w": (640, 512), "ln_b": (1152, 512),
      "w0": (1664, 512)}
NRW = 2176


def _col(v):
    return np.ascontiguousarray(np.asarray(v, np.float32).reshape(-1, 128).T)


def build(layers=(0, 1), stages=("dsa", "rwkv", "nsa", "ffn"), dbg=False):
    nc = bass.Bass("TRN2", target_bir_lowering=False)
    dr = {}

    def din(name, shape):
        dr[name] = nc.dram_tensor(name, list(shape), F32, kind="ExternalInput").ap()
        return dr[name]

    x_d = din("x", [T, D])
    ccol_d = din("ccol", [128, 8])
    cf_d = din("cfd", [128, NCF])
    vec_d = din("vec", [DEPTH, 128, NVC])
    row_d = din("row", [DEPTH, NRW])
    ada_w = din("ada_w", [DEPTH, D, 6 * D])
    w_in = din("w_in", [DEPTH, D, 3152])
    w_out = din("w_out", [DEPTH, D, D])
    ffn_wi = din("ffn_wi", [DEPTH, D, 2 * FFN_H])
    ffn_wo = din("ffn_wo", [DEPTH, FFN_H, D])
    rw_w2 = din("rwkv_w2", [DEPTH, 64, 512])
    rw_a2 = din("rwkv_a2", [DEPTH, 64, 512])
    rw_g2 = din("rwkv_g2", [DEPTH, 128, 512])
    nsa_w1 = din("nsa_w1", [DEPTH, 2, 2048, 256])
    nsa_w2 = din("nsa_w2", [DEPTH, 2, 256, 64])
    y_d = nc.dram_tensor("y", [T, D], F32, kind="ExternalOutput").ap()
    if dbg:
        dbg_d = nc.dram_tensor("dbg", [128, 8192], F32, kind="ExternalOutput").ap()

    with ExitStack() as es:
        E = es.enter_context
        sems = {e: E(nc.semaphore("s_" + e)) for e in ("pe", "act", "dve", "pool")}
        dsems = [E(nc.semaphore(f"dq{i}")) for i in range(24)]
        P = Prog(nc, sems, dsems)

        uid = [0]

        def sb(stack, name, shape, dt=F32, nsub=1):
            uid[0] += 1
            name = f"{name}_{uid[0]}"
            return Buf(name, stack.enter_context(nc.sbuf_tensor("s_" + name, list(shape), dt)), nsub)

        xT = sb(es, "xT", [128, 8, T], F32, nsub=NT)
        hT = sb(es, "hT", [128, 8, T], BF16, nsub=NT)
        cf = sb(es, "cf", [128, 256], F32)
        identb = sb(es, "identb", [128, 128], BF16)
        onesr = sb(es, "onesr", [128, 128], F32)
        modT = sb(es, "modT", [128, 48], F32)
        A1 = sb(es, "A1", [128, 8], F32)
        A2 = sb(es, "A2", [128, 8], F32)
        vec = sb(es, "vec", [128, NVC], F32)
        ccol = sb(es, "ccol", [128, 8], F32)
        PS = [Buf(f"ps{i}", E(nc.psum_tensor(f"ps{i}", [128, 512], F32))) for i in range(8)]
        PSB = [p.t[:].bitcast(BF16) for p in PS]

        identf = cf[:, 0:128]
        onesf = cf[:, 128:256]

        def loadc(stack, name, dt=F32, rows=128, stage=None):
            o, wd = CF[name]
            t32 = sb(stack if (stage is None or dt == F32) else stage, "c32_" + name, [128, wd], F32)
            P.dma(t32[0:rows, :], cf_d[0:rows, o:o + wd], w=[t32])
            if dt == F32:
                return t32
            tb = sb(stack, "cb_" + name, [128, wd], dt)
            P.cp("pool", tb[0:rows, :], t32[0:rows, :], r=[t32], w=[tb])
            return tb

        def tkeys(buf, tg):
            return (buf, range(4 * tg, 4 * tg + 4))

        with ExitStack() as ph:
            xin = [sb(ph, f"xin{i}", [128, D]) for i in range(2)]
            P.dma(cf[:, :], cf_d[:, 0:256], w=[cf])
            P.dma(ccol[:, :], ccol_d, w=[ccol])
            P.cp("dve", identb[:, :], identf, r=[cf], w=[identb])
            P.cp("dve", RR(onesr[:, :]), onesf, r=[cf], w=[onesr])
            for tt in range(NT):
                xi = xin[tt % 2]
                P.dma(xi[:, :], x_d[tt * 128:(tt + 1) * 128, :], w=[xi])
                for half in range(2):
                    bank = PS[(tt * 2 + half) % 4]
                    for q in range(4):
                        fc = half * 4 + q
                        P.tr(bank[:, q * 128:(q + 1) * 128], xi[:, fc * 128:(fc + 1) * 128], identf,
                             r=[xi, cf], w=[bank])
                    P.cp("act" if half == 0 else "dve",
                         xT[:, half * 4:half * 4 + 4, tt * 128:(tt + 1) * 128],
                         bank[:, :].rearrange("p (a b) -> p a b", a=4), r=[bank], w=[(xT, tt)])
            P.flush()

        def rmsnorm_mod(Acol, shcol):
            with ExitStack() as ph:
                sq = [sb(ph, f"nsq{i}", [128, 512]) for i in range(2)]
                tmp = [sb(ph, f"ntmp{i}", [128, 512]) for i in range(2)]
                rstd = sb(ph, "nrstd", [128, 512])
                for tg in range(4):
                    sl = slice(tg * 512, (tg + 1) * 512)
                    bank = PS[tg % 2]
                    for fc in range(8):
                        s = sq[fc % 2]
                        P.act(RR(s[:, :]), xT[:, fc, sl], AF.Square, r=[tkeys(xT, tg)], w=[s])
                        P.mm(bank[:, :], RR(onesr[:, :]), RR(s[:, :]), start=(fc == 0), stop=(fc == 7), r=[s, onesr], w=[bank])
                    P.act(rstd[:, :], bank[:, :], AF.Sqrt, r=[bank], w=[rstd], bias=1e-6, scale=1.0 / D)
                    P.recip(rstd[:, :], rstd[:, :], r=[rstd], w=[rstd])
                    for fc in range(8):
                        tb = tmp[fc % 2]
                        P.tt("dve", tb[:, :], xT[:, fc, sl], rstd[:, :], ALU.mult, r=[tkeys(xT, tg), rstd], w=[tb])
                        P.act(hT[:, fc, sl], tb[:, :], AF.Identity, r=[tb, modT, A1, A2], w=[tkeys(hT, tg)],
                              bias=shcol(fc), scale=Acol(fc))
                P.flush()

        def outproj_add(srcT, nkc, wrow0, gcol0, l, tag):
            with ExitStack() as ph:
                wo32 = sb(ph, "wo32" + tag, [128, nkc, D])
                wo = sb(ph, "wo" + tag, [128, nkc, D], BF16)
                P.dma(wo32[:, :, :], w_out[l, wrow0:wrow0 + nkc * 128, :].rearrange("(k p) n -> p k n", p=128), w=[wo32])
                P.cp("act", wo[:, :, :], wo32[:, :, :], r=[wo32], w=[wo])
                cnt = 0
                for tg in range(4):
                    sl = slice(tg * 512, (tg + 1) * 512)
                    for fc in range(8):
                        bank = PS[cnt % 4]
                        cnt += 1
                        for kc in range(nkc):
                            P.mm(bank[:, :], wo[:, kc, fc * 128:(fc + 1) * 128], srcT[:, kc, sl],
                                 start=(kc == 0), stop=(kc == nkc - 1), r=[wo, srcT], w=[bank])
                        P.stt(xT[:, fc, sl], bank[:, :], modT[:, gcol0 + fc:gcol0 + fc + 1], xT[:, fc, sl],
                              ALU.mult, ALU.add, r=[bank, modT, tkeys(xT, tg)], w=[tkeys(xT, tg)])
                P.flush()

        for l in layers:
            with ExitStack() as ph:
                awb = [sb(ph, f"awb{i}", [128, 8, 512]) for i in range(2)]
                sc = sb(ph, "silc", [128, 8, 2])
                P.dma(vec[:, :], vec_d[l], w=[vec])
                P.act(sc[:, :, 0], ccol[:, :], AF.Silu, r=[ccol], w=[sc])
                P.act(sc[:, :, 1], ccol[:, :], AF.Silu, r=[ccol], w=[sc])
                psA = PS[0]
                for nb in range(12):
                    ab = awb[nb % 2]
                    P.dma(ab[:, :, :], ada_w[l, :, nb * 512:(nb + 1) * 512].rearrange("(k p) n -> p k n", p=128), w=[ab])
                    for q in range(4):
                        j = nb * 4 + q
                        for kc in range(8):
                            P.mm(psA[:, 2 * j:2 * j + 2], ab[:, kc, q * 128:(q + 1) * 128], sc[:, kc, :],
                                 start=(kc == 0), stop=(kc == 7), r=[ab, sc], w=[psA], skip=True)
                P.tt("dve", modT[:, :], psA[:, 0:96].rearrange("p (j two) -> p j two", two=2)[:, :, 0],
                     vec[:, 0:48], ALU.add, r=[psA, vec], w=[modT])
                o1 = VC["n1g"][0]
                o2 = VC["n2g"][0]
                P.stt(A1[:, :], modT[:, 8:16], 1.0, vec[:, o1:o1 + 8], ALU.add, ALU.mult, r=[modT, vec], w=[A1])
                P.stt(A2[:, :], modT[:, 32:40], 1.0, vec[:, o2:o2 + 8], ALU.add, ALU.mult, r=[modT, vec], w=[A2])
                P.flush()

            rmsnorm_mod(lambda fc: A1[:, fc:fc + 1], lambda fc: modT[:, fc:fc + 1])
            if "dsa" in stages:
                dsa_phase(nc, P, l, locals())
            if "rwkv" in stages:
                rwkv_phase(nc, P, l, locals())
            if "nsa" in stages:
                nsa_phase(nc, P, l, locals())

            if "ffn" in stages:
                rmsnorm_mod(lambda fc: A2[:, fc:fc + 1], lambda fc: modT[:, 24 + fc:25 + fc])
                with ExitStack() as ph:
                    wi32 = [sb(ph, f"wi32{i}", [128, 8, 256]) for i in range(2)]
                    wib = [sb(ph, f"wib{i}", [128, 8, 256], BF16, nsub=2) for i in range(2)]
                    wo32 = sb(ph, "fwo32", [128, 22, 128])
                    wob = [sb(ph, f"fwob{i}", [128, 22, 128], BF16, nsub=2) for i in range(2)]
                    actT = sb(ph, "actT", [128, 22, 1024], BF16, nsub=44)
                    sg = [sb(ph, f"fsg{i}", [128, 512]) for i in range(2)]
                    cnt = 0
                    for tp in range(2):
                        for hc in range(22):
                            w32 = wi32[cnt % 2]
                            wb = wib[cnt % 2]
                            P.dma(w32[:, :, 0:128], ffn_wi[l, :, hc * 128:(hc + 1) * 128].rearrange("(k p) n -> p k n", p=128), w=[w32])
                            P.dma(w32[:, :, 128:256], ffn_wi[l, :, FFN_H + hc * 128:FFN_H + (hc + 1) * 128].rearrange("(k p) n -> p k n", p=128), w=[w32])
                            P.cp("pool", wb[:, 0:5, :], w32[:, 0:5, :], r=[w32], w=[(wb, 0)])
                            P.cp("act", wb[:, 5:8, :], w32[:, 5:8, :], r=[w32], w=[(wb, 1)])
                            for u in range(2):
                                tg = 2 * tp + u
                                sl = slice(tg * 512, (tg + 1) * 512)
                                bg = PS[(cnt % 2) * 4 + 2 * u]
                                bu = PS[(cnt % 2) * 4 + 2 * u + 1]
                                for kc in range(8):
                                    P.mm(bg[:, :], wb[:, kc, 0:128], hT[:, kc, sl], start=(kc == 0), stop=(kc == 7),
                                         r=[wb, tkeys(hT, tg)], w=[bg])
                                for kc in range(8):
                                    P.mm(bu[:, :], wb[:, kc, 128:256], hT[:, kc, sl], start=(kc == 0), stop=(kc == 7),
                                         r=[wb, tkeys(hT, tg)], w=[bu])
                                s_ = sg[u]
                                P.act(s_[:, :], bg[:, :], AF.Silu, r=[bg], w=[s_])
                                P.tt("dve", actT[:, hc, u * 512:(u + 1) * 512], s_[:, :], bu[:, :], ALU.mult, r=[s_, bu],
                                     w=[(actT, 2 * hc + u)])
                            cnt += 1
                        for fc in range(8):
                            wb = wob[fc % 2]
                            P.dma(wo32[:, :, :], ffn_wo[l, :, fc * 128:(fc + 1) * 128].rearrange("(k p) n -> p k n", p=128),
                                  r=[wo32], w=[wo32])
                            P.cp("pool", wb[:, 0:13, :], wo32[:, 0:13, :], r=[wo32], w=[(wb, 0)])
                            P.cp("act", wb[:, 13:22, :], wo32[:, 13:22, :], r=[wo32], w=[(wb, 1)])
                            for u in range(2):
                                tg = 2 * tp + u
                                sl = slice(tg * 512, (tg + 1) * 512)
                                bank = PS[(2 * fc + u) % 4]
                                for hc in range(22):
                                    P.mm(bank[:, :], wb[:, hc, :], actT[:, hc, u * 512:(u + 1) * 512], start=(hc == 0), stop=(hc == 21),
                                         r=[wb, (actT, 2 * hc + u)], w=[bank])
                                P.stt(xT[:, fc, sl], bank[:, :], modT[:, 40 + fc:41 + fc], xT[:, fc, sl],
                                      ALU.mult, ALU.add, r=[bank, modT, tkeys(xT, tg)], w=[tkeys(xT, tg)])
                    P.flush()

        with ExitStack() as ph:
            xo = [sb(ph, f"xo{i}", [128, D]) for i in range(2)]
            for tt in range(NT):
                xb = xo[tt % 2]
                for half in range(2):
                    bank = PS[(tt * 2 + half) % 4]
                    for q in range(4):
                        fc = half * 4 + q
                        P.tr(bank[:, q * 128:(q + 1) * 128], xT[:, fc, tt * 128:(tt + 1) * 128], identf,
                             r=[(xT, tt), cf], w=[bank])
                    P.cp("act" if half == 0 else "dve", xb[:, half * 512:(half + 1) * 512], bank[:, :],
                         r=[bank], w=[xb])
                P.dma(y_d[tt * 128:(tt + 1) * 128, :], xb[:, :], r=[xb], w=["y_out"])
            P.flush()
        build.stats = (P.n_ins, P.n_wait)
    return nc


def dsa_phase(nc, P, l, env):
    sb, xT, hT, PS, PSB, loadc = (env[k] for k in ("sb", "xT", "hT", "PS", "PSB", "loadc"))
    identb, w_in, row_d, modT, outproj_add = (env[k] for k in ("identb", "w_in", "row_d", "modT", "outproj_add"))
    NA = 708
    with ExitStack() as ph:
        oaT = sb(ph, "oaT", [128, 2, T], BF16, nsub=NT)
        with ExitStack() as p2:
            wA = sb(p2, "wA", [128, 8, NA], BF16)
            kT = sb(p2, "kT", [64, T], BF16, nsub=NT)
            ikT = sb(p2, "ikT", [64, T], BF16, nsub=NT)
            Vp = sb(p2, "Vp", [128, NT, 65], BF16, nsub=NT)
            qT2 = sb(p2, "qT2", [64, 2, 512], BF16, nsub=2)
            iqT2 = sb(p2, "iqT2", [64, 2, 512], BF16, nsub=2)
            zs = sb(p2, "zs", [128, NA])
            sqv = sb(p2, "sqv", [128, 320])
            ss = sb(p2, "ss", [128, 5])
            rw = sb(p2, "rw", [128, 10, 64])
            ro = sb(p2, "ro", [128, 10, 64], BF16)
            t1 = sb(p2, "t1", [128, 10, 32])
            t2 = sb(p2, "t2", [128, 10, 32])
            t3 = sb(p2, "t3", [128, 10, 32])
            t4 = sb(p2, "t4", [128, 10, 32])
            iwt = sb(p2, "iwt", [128, 2, 4], nsub=2)
            gA = sb(p2, "gA", [128, 320])
            sc = sb(p2, "sc", [128, T])
            wk = sb(p2, "wk", [128, T], BF16)
            lo = sb(p2, "lo", [128, 1])
            mid = sb(p2, "mid", [128, 1])
            cntb = sb(p2, "cntb", [128, 2])
            NBIS = 24
            nm = sb(p2, "nm", [128, 2, T], BF16, nsub=2)
            m8 = sb(p2, "m8", [128, 8])
            rl = sb(p2, "rl", [128, 2, 512], nsub=2)
            pT = sb(p2, "pT", [128, 2, 512], BF16, nsub=2)
            oa = sb(p2, "oa", [128, 256], BF16)
            rec = sb(p2, "rec", [128, 4])
            cs = loadc(p2, "cs")
            cneg = loadc(p2, "cneg")
            cnegb = loadc(p2, "cnegb", BF16)
            i4 = loadc(p2, "i4", BF16)
            with ExitStack() as p3:
                wst = sb(p3, "wAst", [128, 4, NA])
                for hf in range(2):
                    P.dma(wst[:, :, :], w_in[l, hf * 512:(hf + 1) * 512, 0:NA].rearrange("(k p) n -> p k n", p=128),
                          r=[wst], w=[wst])
                    P.cp("act" if hf else "dve", wA[:, hf * 4:(hf + 1) * 4, :], wst[:, :, :], r=[wst], w=[wA])
                P.flush()
            o, w_ = RW["dsa_g"]
            P.dma(gA[:, :], row_d[l:l + 1, o:o + w_].partition_broadcast(128), w=[gA])
            P.memset("pool", Vp[:, :, :], 1.0, w=[Vp])

            import os
            LV = int(os.environ.get("KLV", "9"))

            def prep(i):
                tsl = slice(i * 128, (i + 1) * 128)
                if LV < 2:
                    return
                for kc in range(8):
                    P.mm(PS[0][:, 0:512], hT[:, kc, tsl], wA[:, kc, 0:512], start=(kc == 0), stop=(kc == 7),
                         r=[(hT, i), wA], w=[PS[0]])
                for kc in range(8):
                    P.mm(PS[1][:, 0:NA - 512], hT[:, kc, tsl], wA[:, kc, 512:NA], start=(kc == 0), stop=(kc == 7),
                         r=[(hT, i), wA], w=[PS[1]])
                P.cp("act", zs[:, 0:512], PS[0][:, 0:512], r=[PS[0]], w=[zs])
                P.cp("dve", zs[:, 512:NA], PS[1][:, 0:NA - 512], r=[PS[1]], w=[zs])
                if LV < 3:
                    return
                P.cp("act", Vp[:, i, 0:64], zs[:, 320:384], r=[zs], w=[(Vp, i)])
                P.cp("pool", iwt[:, i % 2, :], zs[:, 704:708], r=[zs], w=[(iwt, i % 2)])
                P.tt("dve", sqv[:, :], zs[:, 0:320], zs[:, 0:320], ALU.mult, r=[zs], w=[sqv])
                P.red(ss[:, :], sqv[:, :].rearrange("p (g d) -> p g d", g=5), ALU.add, r=[sqv], w=[ss])
                P.act(ss[:, :], ss[:, :], AF.Sqrt, r=[ss], w=[ss], bias=1e-6, scale=1.0 / 64)
                P.recip(ss[:, :], ss[:, :], r=[ss], w=[ss])
                P.tt("dve", rw[:, 0:5, :], zs[:, 0:320].rearrange("p (g d) -> p g d", g=5),
                     ss[:, :].unsqueeze(2).to_broadcast([128, 5, 64]), ALU.mult, r=[zs, ss], w=[rw])
                P.tt("dve", rw[:, 0:5, :], rw[:, 0:5, :], gA[:, :].rearrange("p (g d) -> p g d", g=5), ALU.mult,
                     r=[rw, gA], w=[rw])
                P.cp("pool", rw[:, 5:10, :], zs[:, 384:704].rearrange("p (g d) -> p g d", g=5), r=[zs], w=[rw])
                if LV < 4:
                    return
                cosb = cs[:, i * 64:i * 64 + 32].unsqueeze(1).to_broadcast([128, 10, 32])
                sinb = cs[:, i * 64 + 32:i * 64 + 64].unsqueeze(1).to_broadcast([128, 10, 32])
                P.tt("dve", t1[:, :, :], rw[:, :, 0:32], cosb, ALU.mult, r=[rw, cs], w=[t1])
                P.tt("pool", t2[:, :, :], rw[:, :, 32:64], sinb, ALU.mult, r=[rw, cs], w=[t2])
                P.tt("pool", t3[:, :, :], rw[:, :, 32:64], cosb, ALU.mult, r=[rw, cs], w=[t3])
                P.tt("dve", t4[:, :, :], rw[:, :, 0:32], sinb, ALU.mult, r=[rw, cs], w=[t4])
                P.tt("dve", ro[:, :, 0:32], t1[:, :, :], t2[:, :, :], ALU.subtract, r=[t1, t2], w=[ro])
                P.tt("pool", ro[:, :, 32:64], t3[:, :, :], t4[:, :, :], ALU.add, r=[t3, t4], w=[ro])
                if LV < 5:
                    return
                SUB = os.environ.get("KSUB", "qik")
                EV = os.environ.get("KEV", "dve")
                if "q" in SUB:
                    for g in range(4):
                        P.tr(PSB[0][0:64, g * 128:(g + 1) * 128], ro[:, g, :], identb[:, :], r=[ro, identb], w=[PS[0]])
                    P.cp(EV, qT2[:, i % 2, :], PSB[0][0:64, 0:512], r=[PS[0]], w=[(qT2, i % 2)])
                if "i" in SUB:
                    for g in range(4):
                        P.tr(PSB[0][0:64, 512 + g * 128:512 + (g + 1) * 128], ro[:, 5 + g, :], identb[:, :],
                             r=[ro, identb], w=[PS[0]])
                    P.cp("dve", iqT2[:, i % 2, :], PSB[0][0:64, 512:1024], r=[PS[0]], w=[(iqT2, i % 2)])
                if "k" in SUB:
                    P.tr(PSB[1][0:64, 0:128], ro[:, 4, :], identb[:, :], r=[ro, identb], w=[PS[1]])
                    P.tr(PSB[1][0:64, 128:256], ro[:, 9, :], identb[:, :], r=[ro, identb], w=[PS[1]])
                    P.cp(EV, kT[:, tsl], PSB[1][0:64, 0:128], r=[PS[1]], w=[(kT, i)])
                    P.cp("dve", ikT[:, tsl], PSB[1][0:64, 128:256], r=[PS[1]], w=[(ikT, i)])

            def select(i):
                L = (i + 1) * 128
                b = i % 2
                if i < 2:
                    if i == 1:
                        P.memset("pool", nm[:, b, 0:128], 0.0, w=[(nm, b)])
                    P.cp("pool", nm[:, b, i * 128:L], cnegb[:, :], r=[cnegb], w=[(nm, b)])
                    return
                cnt = 0
                for c0 in range(0, L, 512):
                    c1 = min(L, c0 + 512)
                    wd = c1 - c0
                    tiles = range(c0 // 128, c1 // 128)
                    for h in range(4):
                        bank = PS[2 + cnt % 2]
                        r_ = rl[:, cnt % 2, 0:wd]
                        P.mm(bank[:, 0:wd], iqT2[:, b, h * 128:(h + 1) * 128], ikT[:, c0:c1],
                             r=[(iqT2, b), (ikT, tiles)], w=[bank])
                        P.act(r_, bank[:, 0:wd], AF.Relu, r=[bank], w=[(rl, cnt % 2)])
                        if h == 0:
                            P.ts("dve", sc[:, c0:c1], r_, iwt[:, b, 0:1], ALU.mult, r=[(rl, cnt % 2), (iwt, b)], w=[sc])
                        else:
                            P.stt(sc[:, c0:c1], r_, iwt[:, b, h:h + 1], sc[:, c0:c1], ALU.mult, ALU.add,
                                  r=[(rl, cnt % 2), (iwt, b), sc], w=[sc])
                        cnt += 1
                P.tt("dve", sc[:, i * 128:L], sc[:, i * 128:L], cneg[:, :], ALU.add, r=[sc, cneg], w=[sc])
                P.memset("dve", mid[:, :], 0.0, w=[mid])
                for n_ in range(NBIS):
                    w_next = 2048.0 / (2 ** (n_ + 1))
                    P.op("dve", (lambda L_: lambda e: e.tensor_scalar(out=wk[:, 0:L_], in0=sc[:, 0:L_], scalar1=mid[:, 0:1],
                                                                      scalar2=0.0, op0=ALU.is_ge, op1=ALU.add,
                                                                      accum_out=cntb[:, 0:1]))(L), r=[sc, mid], w=[wk, cntb])
                    if n_ < NBIS - 1:
                        P.ts("dve", cntb[:, 1:2], cntb[:, 0:1], 255.5, ALU.is_ge, 2.0 * w_next, ALU.mult, r=[cntb], w=[cntb])
                        P.stt(mid[:, :], cntb[:, 1:2], -w_next, mid[:, :], ALU.add, ALU.add, r=[cntb, mid], w=[mid])
                    else:
                        w_last = 2048.0 / (2 ** n_)
                        P.ts("dve", cntb[:, 1:2], cntb[:, 0:1], 255.5, ALU.is_ge, -1.0, ALU.add, r=[cntb], w=[cntb])
                        P.stt(lo[:, :], cntb[:, 1:2], w_last, mid[:, :], ALU.mult, ALU.add, r=[cntb, mid], w=[lo])
                P.ts("dve", nm[:, b, 0:L], sc[:, 0:L], lo[:, 0:1], ALU.is_lt, -BIG, ALU.mult, r=[sc, lo], w=[(nm, b)])

            def attend(i):
                b = i % 2
                tsl = slice(i * 128, (i + 1) * 128)
                for j in range(i + 1):
                    bank = PS[2 + j % 2]
                    P.mm(bank[:, :], kT[:, j * 128:(j + 1) * 128], qT2[:, b, :], start=True, stop=False,
                         r=[(kT, j), (qT2, b)], w=[bank])
                    P.mm(bank[:, :], nm[:, b, j * 128:(j + 1) * 128], i4[:, :], start=False, stop=True,
                         r=[(nm, b), i4], w=[bank])
                    P.act(pT[:, j % 2, :], bank[:, :], AF.Exp, r=[bank], w=[(pT, j % 2)], scale=0.125)
                    for h in range(4):
                        P.mm(PS[4 + h][:, 0:65], pT[:, j % 2, h * 128:(h + 1) * 128], Vp[:, j, :],
                             start=(j == 0), stop=(j == i), r=[(pT, j % 2), (Vp, j)], w=[PS[4 + h]])
                for h in range(4):
                    P.recip(rec[:, h:h + 1], PS[4 + h][:, 64:65], r=[PS[4 + h]], w=[rec])
                    P.act(oa[:, h * 64:(h + 1) * 64], PS[4 + h][:, 0:64], AF.Copy, r=[PS[4 + h], rec], w=[oa],
                          scale=rec[:, h:h + 1])
                for c in range(2):
                    P.tr(PSB[1][:, 512 + c * 128:512 + (c + 1) * 128], oa[:, c * 128:(c + 1) * 128], identb[:, :],
                         r=[oa, identb], w=[PS[1]])
                P.cp("dve", oaT[:, :, tsl], PSB[1][:, 512:768].rearrange("p (c t) -> p c t", c=2), r=[PS[1]],
                     w=[(oaT, i)])

            import os
            mode = os.environ.get("KDBG", "full")
            if mode == "full":
                prep(0)
                select(0)
                for i in range(NT):
                    if i + 1 < NT:
                        prep(i + 1)
                        select(i + 1)
                    attend(i)
            elif mode == "prep":
                for i in range(NT):
                    prep(i)
            elif mode == "sel":
                for i in range(4):
                    prep(i)
                    select(i)
            elif mode == "att":
                for i in range(2):
                    prep(i)
                    select(i)
                    attend(i)
            P.flush()
        if mode == "full":
            outproj_add(oaT, 2, 0, 16, l, "A")


def rwkv_phase(nc, P, l, env):
    sb, xT, hT, PS, PSB, loadc = (env[k] for k in ("sb", "xT", "hT", "PS", "PSB", "loadc"))
    w_in, w_out, row_d, modT, vec, cf = (env[k] for k in ("w_in", "w_out", "row_d", "modT", "vec", "cf"))
    identf, onesf = env["identf"], env["onesf"]
    rw_w2, rw_a2, rw_g2 = env["rw_w2"], env["rw_a2"], env["rw_g2"]
    NB = 1792
    C0 = 708
    with ExitStack() as ph:
        wB = sb(ph, "wB", [128, 8, NB], BF16)
        woB = sb(ph, "woB", [128, 4, D], BF16)
        with ExitStack() as p3:
            wst = sb(p3, "wBst", [128, 2, NB])
            for q in range(4):
                P.dma(wst[:, :, :], w_in[l, q * 256:(q + 1) * 256, C0:C0 + NB].rearrange("(k p) n -> p k n", p=128),
                      r=[wst], w=[wst])
                P.cp("act" if q % 2 else "dve", wB[:, 2 * q:2 * q + 2, :], wst[:, :, :], r=[wst], w=[wB])
            for q in range(2):
                P.dma(wst[:, :, 0:D], w_out[l, 256 + q * 256:256 + (q + 1) * 256, :].rearrange("(k p) n -> p k n", p=128),
                      r=[wst], w=[wst])
                P.cp("act" if q % 2 else "dve", woB[:, 2 * q:2 * q + 2, :], wst[:, :, 0:D], r=[wst], w=[woB])
            P.flush()
        wa2 = sb(ph, "wa2", [128, 512])
        g2t = sb(ph, "g2t", [128, 512])
        w0r = sb(ph, "w0r", [1, 512])
        lnw = sb(ph, "lnw", [128, 512])
        lnb = sb(ph, "lnb", [128, 512])
        omka = sb(ph, "omka", [128, 4])
        zraw = sb(ph, "zraw", [128, 14, 129])
        zd = sb(ph, "zd", [128, 14, 128])
        twd = sb(ph, "twd", [64, 128])
        sgd = sb(ph, "sgd", [128, 128])
        sgw = sb(ph, "sgw", [128, 512])
        g_tm = sb(ph, "g_tm", [128, 512])
        tri3 = loadc(ph, "tri3")
        ms = loadc(ph, "ms")
        msl = loadc(ph, "msl")
        mi = loadc(ph, "mi")
        bd2 = loadc(ph, "bd2")
        ind2 = loadc(ph, "ind2")
        i2 = loadc(ph, "i2")
        def lane_bufs():
            d = {}
            d["ex3"] = sb(ph, "ex3", [128, 384])
            d["exn"] = sb(ph, "exn", [128, 128])
            d["aT"] = sb(ph, "aT", [128, 128])
            d["kkr"] = sb(ph, "kkr", [128, 128])
            d["sq"] = sb(ph, "rsq", [128, 128])
            d["rn"] = sb(ph, "rn", [128, 128])
            d["kk"] = sb(ph, "kk", [128, 128])
            d["tf"] = sb(ph, "tf", [128, 128])
            d["kp"] = sb(ph, "kp", [128, 128])
            d["bp"] = sb(ph, "bp", [128, 128])
            d["pr"] = sb(ph, "pr", [128, 128])
            d["AT"] = sb(ph, "AT", [128, 128])
            d["BT"] = sb(ph, "BT", [128, 128])
            d["KT"] = sb(ph, "KT", [128, 128])
            d["RT"] = sb(ph, "RT", [128, 128])
            d["KbT"] = sb(ph, "KbT", [128, 128])
            d["BbT"] = sb(ph, "BbT", [128, 128])
            d["tm"] = sb(ph, "tm", [128, 4, 128])
            d["Nm"] = sb(ph, "Nm", [128, 2, 128])
            d["Lm"] = sb(ph, "Lm", [128, 2, 128])
            d["S1"] = sb(ph, "S1", [128, 2, 128])
            d["NRB"] = sb(ph, "NRB", [128, 2, 128])
            d["X"] = sb(ph, "X", [128, 2, 128])
            d["RhT"] = sb(ph, "RhT", [128, 128])
            d["Y0s"] = sb(ph, "Y0s", [128, 128])
            d["GT"] = sb(ph, "GT", [128, 2, 64])
            d["Hs"] = sb(ph, "Hs", [128, 2, 64])
            return d
        LB = [lane_bufs(), None]
        with_core = LB[0]
        LB[1] = {k_: (v_ if k_ in ("Nm", "Lm", "S1", "NRB", "X", "RhT", "Y0s", "GT", "Hs") else None) for k_, v_ in with_core.items()}
        for k_ in list(LB[1]):
            if LB[1][k_] is None:
                shp = {"ex3": [128, 384], "tm": [128, 4, 128]}.get(k_, [128, 128])
                LB[1][k_] = sb(ph, k_ + "_l1", shp)
        NAMES = ['ex3', 'exn', 'aT', 'kkr', 'sq', 'rn', 'kk', 'tf', 'kp', 'bp', 'pr', 'AT', 'BT', 'KT', 'RT', 'KbT', 'BbT', 'tm', 'Nm', 'Lm', 'S1', 'NRB', 'X', 'RhT', 'Y0s', 'GT', 'Hs']
        Mst = sb(ph, "Mst", [128, 4, 64], nsub=4)
        y_tm = sb(ph, "y_tm", [128, 512], nsub=4)
        vtm = sb(ph, "vtm", [128, 512], nsub=4)
        ysq = sb(ph, "ysq", [128, 512])
        st = sb(ph, "gnst", [128, 4, 8])
        bon = sb(ph, "bon", [128, 8])
        obT = sb(ph, "obT", [128, 4, 128], BF16)

        P.dma(wa2[0:64, :], rw_w2[l], w=[wa2])
        P.dma(wa2[64:128, :], rw_a2[l], w=[wa2])
        P.dma(g2t[:, :], rw_g2[l], w=[g2t])
        o, w_ = RW["w0"]
        P.dma(w0r[0:1, :], row_d[l:l + 1, o:o + w_], w=[w0r])
        o, w_ = RW["ln_w"]
        P.dma(lnw[:, :], row_d[l:l + 1, o:o + w_].partition_broadcast(128), w=[lnw])
        o, w_ = RW["ln_b"]
        P.dma(lnb[:, :], row_d[l:l + 1, o:o + w_].partition_broadcast(128), w=[lnb])
        oka = VC["k_a"][0]
        okk = VC["k_k"][0]
        oa0 = VC["a0"][0]
        ork = VC["r_k"][0]
        omu = VC["mu"][0]
        P.ts("dve", omka[:, :], vec[:, oka:oka + 4], -1.0, ALU.mult, 1.0, ALU.add, r=[vec], w=[omka])
        P.memset("pool", zraw[:, :, :], 0.0, w=[zraw])
        P.memset("pool", Mst[:, :, :], 0.0, w=[Mst])
        mub = vec[:, omu:omu + 14].unsqueeze(2).to_broadcast([128, 14, 128])
        rot = [0]

        def nb():
            rot[0] += 1
            return PS[rot[0] % 4]

        import os
        RL = int(os.environ.get("KRL", "9"))
        for i in range(NT):
            tsl = slice(i * 128, (i + 1) * 128)
            if RL < 2:
                continue
            for f in range(14):
                bank = PS[f // 4]
                for kc in range(8):
                    P.mm(bank[:, (f % 4) * 128:(f % 4 + 1) * 128], wB[:, kc, f * 128:(f + 1) * 128], hT[:, kc, tsl],
                         start=(kc == 0), stop=(kc == 7), r=[wB, (hT, i)], w=[bank])
            for b4 in range(4):
                n_ = 4 if b4 < 3 else 2
                P.cp("act" if b4 % 2 == 0 else "dve", zraw[:, b4 * 4:b4 * 4 + n_, 1:129],
                     PS[b4][:, 0:n_ * 128].rearrange("p (a b) -> p a b", a=n_), r=[PS[b4]], w=[zraw])
            P.tt("dve", zd[:, :, :], zraw[:, :, 0:128], zraw[:, :, 1:129], ALU.subtract, r=[zraw], w=[zd])
            P.tt("dve", zd[:, :, :], zd[:, :, :], mub, ALU.mult, r=[zd, vec], w=[zd])
            P.tt("dve", zd[:, :, :], zd[:, :, :], zraw[:, :, 1:129], ALU.add, r=[zd, zraw], w=[zd])
            P.cp("pool", zraw[:, :, 0:1], zraw[:, :, 128:129], r=[zraw, zd], w=[zraw])
            if RL < 3:
                continue
            P.act(twd[:, :], zd[0:64, 12, :], AF.Tanh, r=[zd], w=[twd])
            P.act(sgd[:, :], zd[:, 13, :], AF.Sigmoid, r=[zd], w=[sgd])
            P.mm(PS[4][:, :], twd[:, :], wa2[0:64, :], start=True, stop=False, r=[twd, wa2], w=[PS[4]])
            P.mm(PS[4][:, :], onesf[0:1, 0:128], w0r[0:1, :], start=False, stop=True, r=[cf, w0r], w=[PS[4]])
            P.act(sgw[:, :], PS[4][:, :], AF.Sigmoid, r=[PS[4]], w=[sgw])
            P.mm(PS[5][:, :], sgd[:, :], g2t[:, :], r=[sgd, g2t], w=[PS[5]])
            P.cp("dve", g_tm[:, :], PS[5][:, :], r=[PS[5]], w=[g_tm])
            def hc_gen(hc, B):
                ex3, exn, aT, kkr, sq, rn, kk, tf, kp, bp, pr, AT, BT, KT, RT, KbT, BbT, tm, Nm, Lm, S1, NRB, X, RhT, Y0s, GT, Hs = (B[n] for n in NAMES)
                if RL < 4:
                    return
                rT = zd[:, hc, :]
                kTt = zd[:, 4 + hc, :]
                vT = zd[:, 8 + hc, :]
                P.mm(PS[7][:, 0:384], sgw[:, hc * 128:(hc + 1) * 128], tri3[:, :], r=[sgw, tri3], w=[PS[7]])
                P.act(ex3[:, :], PS[7][:, 0:384], AF.Exp, r=[PS[7]], w=[ex3])
                P.act(exn[:, :], PS[7][:, 0:128], AF.Exp, r=[PS[7]], w=[exn], scale=-1.0)
                P.mm(PS[6][:, 0:128], wa2[64:128, hc * 128:(hc + 1) * 128], zd[64:128, 12, :], r=[wa2, zd], w=[PS[6]])
                P.act(aT[:, :], PS[6][:, 0:128], AF.Sigmoid, r=[PS[6], vec], w=[aT], bias=vec[:, oa0 + hc:oa0 + hc + 1])
                yield
                P.ts("dve", kkr[:, :], kTt, vec[:, okk + hc:okk + hc + 1], ALU.mult, r=[zd, vec], w=[kkr])
                P.tt("pool", sq[:, :], kkr[:, :], kkr[:, :], ALU.mult, r=[kkr], w=[sq])
                P.mm(PS[6][:, 128:256], bd2[:, :], sq[:, :], r=[bd2, sq], w=[PS[6]])
                P.act(rn[:, :], PS[6][:, 128:256], AF.Sqrt, r=[PS[6]], w=[rn])
                P.ts("dve", rn[:, :], rn[:, :], 1e-12, ALU.max, r=[rn], w=[rn])
                P.recip(rn[:, :], rn[:, :], r=[rn], w=[rn])
                P.tt("dve", kk[:, :], kkr[:, :], rn[:, :], ALU.mult, r=[kkr, rn], w=[kk])
                yield
                P.ts("dve", tf[:, :], aT[:, :], vec[:, oka + hc:oka + hc + 1], ALU.mult, omka[:, hc:hc + 1], ALU.add,
                     r=[aT, vec, omka], w=[tf])
                P.tt("pool", kp[:, :], kTt, tf[:, :], ALU.mult, r=[zd, tf], w=[kp])
                P.stt(bp[:, :], aT[:, :], -1.0, kk[:, :], ALU.mult, ALU.mult, r=[aT, kk], w=[bp])
                yield
                P.tt("dve", RR(AT[:, :]), kk[:, :], ex3[:, 128:256], ALU.mult, r=[kk, ex3], w=[AT])
                P.tt("dve", RR(BT[:, :]), bp[:, :], exn[:, :], ALU.mult, r=[bp, exn], w=[BT])
                P.tt("dve", RR(KT[:, :]), kp[:, :], exn[:, :], ALU.mult, r=[kp, exn], w=[KT])
                P.tt("dve", RR(RT[:, :]), rT, ex3[:, 0:128], ALU.mult, r=[zd, ex3], w=[RT])
                P.tt("dve", KbT[:, :], kp[:, :], ex3[:, 256:384], ALU.mult, r=[kp, ex3], w=[KbT])
                P.tt("pool", BbT[:, :], bp[:, :], ex3[:, 256:384], ALU.mult, r=[bp, ex3], w=[BbT])
                yield
                P.stt(pr[:, :], rT, vec[:, ork + hc:ork + hc + 1], kp[:, :], ALU.mult, ALU.mult, r=[zd, vec, kp], w=[pr])
                P.mm(PS[5][:, 2 * hc:2 * hc + 2], pr[:, :], ind2[:, :], r=[pr, ind2], w=[PS[5]], skip=True)
                if RL < 5:
                    return
                yield
                for k_, src in enumerate((AT[:, :], KbT[:, :], BbT[:, :], vT)):
                    P.tr(PS[4][:, k_ * 128:(k_ + 1) * 128], src, identf, r=[AT, KbT, BbT, zd, cf], w=[PS[4]])
                P.cp("act", RR(tm[:, :, :]), PS[4][:, :].rearrange("p (a b) -> p a b", a=4), r=[PS[4]], w=[tm])
                P.cp("pool", vtm[:, hc * 128:(hc + 1) * 128], tm[:, 3, :], r=[tm], w=[(vtm, hc)])
                if RL < 6:
                    return
                yield "core"
                def pair_mm(dst, lhs, rhs, mask, eng="dve"):
                    for hh in range(2):
                        po = hh * 64
                        bank = nb()
                        P.mm(bank[:, 0:128], RR(lhs[po:po + 64, :]), RR(rhs[po:po + 64, :]), r=[lhs, rhs], w=[bank])
                        P.tt("dve", RR(dst[:, hh, :]), bank[:, 0:128], mask[:, :], ALU.mult, r=[bank, mask], w=[dst])
                pair_mm(Nm, BT, AT, ms)
                yield
                pair_mm(Lm, AT, BT, msl)
                yield
                pair_mm(S1, KT, AT, ms)
                yield
                pair_mm(NRB, BT, RT, mi)
                yield
                P.cp("dve", RR(X[:, :, 0:64]), tm[:, 0, :].rearrange("p (a b) -> p a b", a=2), r=[tm], w=[X])
                bank = nb()
                for hh in range(2):
                    po = hh * 64
                    P.mm(bank[:, hh * 64:(hh + 1) * 64], RR(S1[:, hh, :]), RR(tm[:, 3, po:po + 64]), r=[S1, tm], w=[bank], skip=True)
                P.cp("act", RR(X[:, :, 64:128]), bank[:, 0:128].rearrange("p (a b) -> p a b", a=2), r=[bank], w=[X])
                pair_mm(S1, KT, RT, mi)
                yield
                for lev in range(6):
                    bx = nb()
                    for hh in range(2):
                        P.mm(bx[:, hh * 128:(hh + 1) * 128], RR(Nm[:, hh, :]), RR(X[:, hh, :]), r=[Nm, X], w=[bx], skip=True)
                    if lev < 5:
                        bn = nb()
                        bl = nb()
                        for hh in range(2):
                            P.mm(bn[:, hh * 128:(hh + 1) * 128], RR(Lm[:, hh, :]), RR(Nm[:, hh, :]), r=[Nm, Lm], w=[bn], skip=True)
                        for hh in range(2):
                            P.mm(bl[:, hh * 128:(hh + 1) * 128], RR(Nm[:, hh, :]), RR(Lm[:, hh, :]), r=[Nm, Lm], w=[bl], skip=True)
                    P.tt("dve", RR(X[:, :, :]), X[:, :, :], bx[:, 0:256].rearrange("p (a b) -> p a b", a=2), ALU.add,
                         r=[X, bx], w=[X])
                    if lev < 5:
                        P.cp("act", RR(Nm[:, :, :]), bn[:, 0:256].rearrange("p (a b) -> p a b", a=2), r=[bn], w=[Nm])
                        P.cp("dve", RR(Lm[:, :, :]), bl[:, 0:256].rearrange("p (a b) -> p a b", a=2), r=[bl], w=[Lm])
                    yield
                bank = nb()
                for hh in range(2):
                    po = hh * 64
                    P.mm(bank[po:po + 64, 0:128], X[:, hh, 0:64], NRB[:, hh, :], r=[X, NRB], w=[bank], skip=True)
                P.tt("dve", RhT[:, :], RT[:, :], bank[:, 0:128], ALU.add, r=[RT, bank], w=[RhT])
                bank = nb()
                for hh in range(2):
                    po = hh * 64
                    P.mm(bank[:, hh * 64:(hh + 1) * 64], RR(S1[:, hh, :]), RR(tm[:, 3, po:po + 64]), start=True, stop=False,
                         r=[S1, tm], w=[bank], skip=True)
                    P.mm(bank[:, hh * 64:(hh + 1) * 64], RR(NRB[:, hh, :]), RR(X[:, hh, 64:128]), start=False, stop=True,
                         r=[NRB, X], w=[bank], skip=True)
                P.cp("act", Y0s[:, :], bank[:, 0:128], r=[bank], w=[Y0s])
                yield
                for c in range(2):
                    cs_ = slice(c * 64, (c + 1) * 64)
                    bg = nb()
                    bh = nb()
                    for hh in range(2):
                        po = hh * 64
                        P.mm(bg[po:po + 64, 0:64], X[cs_, hh, 0:64], tm[cs_, 2, po:po + 64],
                             r=[X, tm], w=[bg], skip=True)
                        P.mm(bh[po:po + 64, 0:64], tm[cs_, 1, po:po + 64], tm[cs_, 3, po:po + 64],
                             start=True, stop=False, r=[tm], w=[bh], skip=True)
                        P.mm(bh[po:po + 64, 0:64], tm[cs_, 2, po:po + 64], X[cs_, hh, 64:128],
                             start=False, stop=True, r=[tm, X], w=[bh], skip=True)
                    P.stt(GT[:, c, :], i2[:, :], ex3[:, 64 * c + 63:64 * c + 64], bg[:, 0:64],
                          ALU.mult, ALU.add, r=[i2, ex3, bg], w=[GT])
                    P.cp("act", Hs[:, c, :], bh[:, 0:64], r=[bh], w=[Hs])
                    yield
                for c in range(2):
                    cs_ = slice(c * 64, (c + 1) * 64)
                    for hh in range(2):
                        po = hh * 64
                        by = nb()
                        bm = nb()
                        P.mm(by[cs_, 0:64], RhT[po:po + 64, cs_], Mst[po:po + 64, hc, :],
                             r=[RhT, (Mst, hc)], w=[by])
                        P.mm(bm[po:po + 64, 0:64], GT[po:po + 64, c, :], Mst[po:po + 64, hc, :],
                             r=[GT, (Mst, hc)], w=[bm])
                        P.tt("dve", y_tm[cs_, hc * 128 + hh * 64:hc * 128 + (hh + 1) * 64], Y0s[cs_, hh * 64:(hh + 1) * 64],
                             by[cs_, 0:64], ALU.add, r=[Y0s, by], w=[(y_tm, hc)])
                        P.tt("dve", Mst[po:po + 64, hc, :], bm[po:po + 64, 0:64], Hs[po:po + 64, c, :], ALU.add,
                             r=[bm, Hs], w=[(Mst, hc)])
                    yield
            gens_ = [hc_gen(h_, LB[h_ % 2]) for h_ in range(4)]

            def to_core(g_):
                for v_ in g_:
                    if v_ == "core":
                        return True
                return False

            ok_ = to_core(gens_[0])
            for h_ in range(4):
                cur_ = gens_[h_] if ok_ else None
                nxt_ = gens_[h_ + 1] if h_ + 1 < 4 else None
                nxt_ready = False
                while cur_ is not None or (nxt_ is not None and not nxt_ready):
                    if cur_ is not None:
                        try:
                            next(cur_)
                        except StopIteration:
                            cur_ = None
                    if nxt_ is not None and not nxt_ready:
                        try:
                            if next(nxt_) == "core":
                                nxt_ready = True
                        except StopIteration:
                            nxt_ = None
                ok_ = nxt_ready
            if RL < 7:
                continue
            y3 = y_tm[:, :].rearrange("p (h d) -> p h d", h=8)
            P.red(st[:, 0, :], y3, ALU.add, r=[y_tm], w=[st])
            P.tt("pool", ysq[:, :], y_tm[:, :], y_tm[:, :], ALU.mult, r=[y_tm], w=[ysq])
            P.red(st[:, 1, :], ysq[:, :].rearrange("p (h d) -> p h d", h=8), ALU.add, r=[ysq], w=[st])
            P.ts("dve", st[:, 0, :], st[:, 0, :], 1.0 / 64, ALU.mult, r=[st], w=[st])
            P.tt("dve", st[:, 2, :], st[:, 0, :], st[:, 0, :], ALU.mult, r=[st], w=[st])
            P.stt(st[:, 1, :], st[:, 1, :], 1.0 / 64, st[:, 2, :], ALU.mult, ALU.subtract, r=[st], w=[st])
            P.act(st[:, 1, :], st[:, 1, :], AF.Sqrt, r=[st], w=[st], bias=64e-5)
            P.recip(st[:, 1, :], st[:, 1, :], r=[st], w=[st])
            P.tt("dve", ysq[:, :].rearrange("p (h d) -> p h d", h=8), y3,
                 st[:, 0, :].unsqueeze(2).to_broadcast([128, 8, 64]), ALU.subtract, r=[y_tm, st], w=[ysq])
            P.tt("dve", ysq[:, :].rearrange("p (h d) -> p h d", h=8), ysq[:, :].rearrange("p (h d) -> p h d", h=8),
                 st[:, 1, :].unsqueeze(2).to_broadcast([128, 8, 64]), ALU.mult, r=[ysq, st], w=[ysq])
            P.tt("pool", ysq[:, :], ysq[:, :], lnw[:, :], ALU.mult, r=[ysq, lnw], w=[ysq])
            P.tt("pool", ysq[:, :], ysq[:, :], lnb[:, :], ALU.add, r=[ysq, lnb], w=[ysq])
            P.cp("act", bon[:, :], PS[5][:, 0:8], r=[PS[5]], w=[bon])
            P.tt("dve", y_tm[:, :].rearrange("p (h d) -> p h d", h=8), vtm[:, :].rearrange("p (h d) -> p h d", h=8),
                 bon[:, :].unsqueeze(2).to_broadcast([128, 8, 64]), ALU.mult, r=[vtm, bon, ysq], w=[y_tm])
            P.tt("dve", ysq[:, :], ysq[:, :], y_tm[:, :], ALU.add, r=[ysq, y_tm], w=[ysq])
            P.tt("dve", ysq[:, :], ysq[:, :], g_tm[:, :], ALU.mult, r=[ysq, g_tm], w=[ysq])
            for k_ in range(4):
                P.tr(PS[6][:, k_ * 128:(k_ + 1) * 128], ysq[:, k_ * 128:(k_ + 1) * 128], identf, r=[ysq, cf], w=[PS[6]])
            P.cp("act", obT[:, :, :], PS[6][:, :].rearrange("p (a b) -> p a b", a=4), r=[PS[6]], w=[obT])
            for fc in range(8):
                bank = PS[fc // 4]
                for kc in range(4):
                    P.mm(bank[:, (fc % 4) * 128:(fc % 4 + 1) * 128], woB[:, kc, fc * 128:(fc + 1) * 128], obT[:, kc, :],
                         start=(kc == 0), stop=(kc == 3), r=[woB, obT], w=[bank])
            for fc in range(8):
                bank = PS[fc // 4]
                P.stt(xT[:, fc, tsl], bank[:, (fc % 4) * 128:(fc % 4 + 1) * 128], modT[:, 16 + fc:17 + fc], xT[:, fc, tsl],
                      ALU.mult, ALU.add, r=[bank, modT, (xT, i)], w=[(xT, i)])
        P.flush()


def nsa_phase(nc, P, l, env):
    sb, xT, hT, PS, PSB, loadc = (env[k] for k in ("sb", "xT", "hT", "PS", "PSB", "loadc"))
    identb, w_in, row_d, modT, vec, outproj_add = (env[k] for k in ("identb", "w_in", "row_d", "modT", "vec", "outproj_add"))
    nsa_w1, nsa_w2 = env["nsa_w1"], env["nsa_w2"]
    NC_ = 652
    C0 = 2500
    ope = VC["pe"][0]
    with ExitStack() as ph:
        ocT = sb(ph, "ocT", [128, 2, T], BF16, nsub=NT)
        with ExitStack() as pa:
            wC = sb(pa, "wC", [128, 8, NC_], BF16)
            ksT = sb(pa, "ksT", [64, T], BF16, nsub=NT)
            kwT = sb(pa, "kwT", [64, T], BF16, nsub=NT)
            Vs = sb(pa, "Vs", [128, NT, 65], BF16, nsub=NT)
            Vw = sb(pa, "Vw", [128, NT, 65], BF16, nsub=NT)
            kcmpT = sb(pa, "kcmpT", [64, 128], BF16)
            vcp = sb(pa, "vcp", [128, 97], BF16)
            gK = sb(pa, "gK", [128, 64])
            gQ = sb(pa, "gQ", [128, 256])
            cs = loadc(pa, "cs")
            cs_default = cs
            addm = loadc(pa, "addm")
            efb = sb(pa, "efb", [128, T], BF16)
            validb = sb(pa, "validb", [128, T], BF16)
            negu4 = sb(pa, "negu4b", [128, 512], BF16)
            negl4 = sb(pa, "negl4b", [128, 512], BF16)
            with ExitStack() as pb:
                wst = sb(pb, "wCst", [128, 4, NC_])
                for hf in range(2):
                    P.dma(wst[:, :, :], w_in[l, hf * 512:(hf + 1) * 512, C0:C0 + NC_].rearrange("(k p) n -> p k n", p=128),
                          r=[wst], w=[wst])
                    P.cp("act" if hf else "dve", wC[:, hf * 4:(hf + 1) * 4, :], wst[:, :, :], r=[wst], w=[wC])
                for dst_, name_, rows_ in ((efb, "efull", 32), (validb, "validT", 127), (negu4, "negu4", 128),
                                           (negl4, "negl4", 128)):
                    o_, wd_ = CF[name_]
                    st_ = sb(pb, "cst_" + name_, [128, wd_])
                    P.dma(st_[0:rows_, :], env["cf_d"][0:rows_, o_:o_ + wd_], w=[st_])
                    P.cp("pool", dst_[0:rows_, :], st_[0:rows_, :], r=[st_], w=[dst_])
                P.flush()
            o, w_ = RW["nsa_kg"]
            P.dma(gK[:, :], row_d[l:l + 1, o:o + w_].partition_broadcast(128), w=[gK])
            o, w_ = RW["nsa_qg"]
            P.dma(gQ[:, :], row_d[l:l + 1, o:o + w_].partition_broadcast(128), w=[gQ])
            P.memset("pool", Vs[:, :, :], 1.0, w=[Vs])
            P.memset("pool", Vw[:, :, :], 1.0, w=[Vw])
            P.memset("pool", vcp[:, :], 0.0, w=[vcp])

            def norm_rope(src, ng, gain, cosb, sinb, dst, tmp, rows=128, tab=None):
                sqv, ssv, ta, tb_, tc, td = tmp
                cs = tab if tab is not None else cs_default
                R = slice(0, rows)
                P.tt("dve", sqv[R, 0:ng, :], src, src, ALU.mult, r=[src_b], w=[sqv])
                P.red(ssv[R, 0:ng], sqv[R, 0:ng, :], ALU.add, r=[sqv], w=[ssv])
                P.act(ssv[R, 0:ng], ssv[R, 0:ng], AF.Sqrt, r=[ssv], w=[ssv], bias=1e-6, scale=1.0 / 64)
                P.recip(ssv[R, 0:ng], ssv[R, 0:ng], r=[ssv], w=[ssv])
                P.tt("dve", src, src, ssv[R, 0:ng].unsqueeze(2).to_broadcast([rows, ng, 64]), ALU.mult, r=[src_b, ssv], w=[src_b])
                P.tt("pool", src, src, gain, ALU.mult, r=[src_b, gK, gQ], w=[src_b])
                P.tt("dve", ta[R, 0:ng, :], src[:, :, 0:32], cosb, ALU.mult, r=[src_b, cs], w=[ta])
                P.tt("pool", tb_[R, 0:ng, :], src[:, :, 32:64], sinb, ALU.mult, r=[src_b, cs], w=[tb_])
                P.tt("pool", tc[R, 0:ng, :], src[:, :, 32:64], cosb, ALU.mult, r=[src_b, cs], w=[tc])
                P.tt("dve", td[R, 0:ng, :], src[:, :, 0:32], sinb, ALU.mult, r=[src_b, cs], w=[td])
                P.tt("dve", dst[:, :, 0:32], ta[R, 0:ng, :], tb_[R, 0:ng, :], ALU.subtract, r=[ta, tb_], w=[dst_b])
                P.tt("pool", dst[:, :, 32:64], tc[R, 0:ng, :], td[R, 0:ng, :], ALU.add, r=[tc, td], w=[dst_b])

            tmp = (sb(pa, "nsq", [128, 4, 64]), sb(pa, "nss", [128, 4]), sb(pa, "nta", [128, 4, 32]),
                   sb(pa, "ntb", [128, 4, 32]), sb(pa, "ntc", [128, 4, 32]), sb(pa, "ntd", [128, 4, 32]))
            rwq = sb(pa, "rwq", [128, 4, 64])
            roq = sb(pa, "roq", [128, 4, 64], BF16)
            with ExitStack() as pb:
                kcT = sb(pb, "kcT", [64, T], BF16, nsub=NT)
                vcT = sb(pb, "vcT", [64, T], BF16, nsub=NT)
                zs = sb(pb, "zsn", [128, 384])
                kvb = sb(pb, "kvb", [128, 2, 64], BF16)
                for i in range(NT):
                    tsl = slice(i * 128, (i + 1) * 128)
                    for kc in range(8):
                        P.mm(PS[0][:, 0:384], hT[:, kc, tsl], wC[:, kc, 256:640], start=(kc == 0), stop=(kc == 7),
                             r=[(hT, i), wC], w=[PS[0]])
                    P.cp("act", zs[:, :], PS[0][:, 0:384], r=[PS[0]], w=[zs])
                    P.cp("pool", kvb[:, :, :], zs[:, 0:128].rearrange("p (g d) -> p g d", g=2), r=[zs], w=[kvb])
                    P.cp("act", Vs[:, i, 0:64], zs[:, 192:256], r=[zs], w=[(Vs, i)])
                    P.cp("act", Vw[:, i, 0:64], zs[:, 320:384], r=[zs], w=[(Vw, i)])
                    P.cp("dve", rwq[:, 0, :], zs[:, 128:192], r=[zs], w=[rwq])
                    P.cp("dve", rwq[:, 1, :], zs[:, 256:320], r=[zs], w=[rwq])
                    src_b, dst_b = rwq, roq
                    norm_rope(rwq[:, 0:2, :], 2, gK[:, :].unsqueeze(1).to_broadcast([128, 2, 64]),
                              cs[:, i * 64:i * 64 + 32].unsqueeze(1).to_broadcast([128, 2, 32]),
                              cs[:, i * 64 + 32:i * 64 + 64].unsqueeze(1).to_broadcast([128, 2, 32]), roq[:, 0:2, :], tmp)
                    P.tr(PSB[1][0:64, 0:128], roq[:, 0, :], identb[:, :], r=[roq, identb], w=[PS[1]])
                    P.tr(PSB[1][0:64, 128:256], roq[:, 1, :], identb[:, :], r=[roq, identb], w=[PS[1]])
                    P.tr(PSB[1][0:64, 256:384], kvb[:, 0, :], identb[:, :], r=[kvb, identb], w=[PS[1]])
                    P.tr(PSB[1][0:64, 384:512], kvb[:, 1, :], identb[:, :], r=[kvb, identb], w=[PS[1]])
                    P.cp("dve", ksT[:, tsl], PSB[1][0:64, 0:128], r=[PS[1]], w=[(ksT, i)])
                    P.cp("dve", kwT[:, tsl], PSB[1][0:64, 128:256], r=[PS[1]], w=[(kwT, i)])
                    P.cp("dve", kcT[:, tsl], PSB[1][0:64, 256:384], r=[PS[1]], w=[(kcT, i)])
                    P.cp("dve", vcT[:, tsl], PSB[1][0:64, 384:512], r=[PS[1]], w=[(vcT, i)])
                blk = sb(pb, "blk", [64, 32, 127], BF16)
                w1s = [sb(pb, f"w1s{q}", [64, 8, 256]) for q in range(2)]
                w1b = [sb(pb, f"w1b{q}", [64, 8, 256], BF16) for q in range(2)]
                w2s = sb(pb, "w2s", [128, 2, 64])
                w2b = sb(pb, "w2b", [128, 2, 64], BF16)
                GT_ = sb(pb, "GTc", [128, 2, 127], BF16)
                gx2 = sb(pb, "gx2", [128, 127])
                gu = sb(pb, "gu", [128, 127])
                cmo = sb(pb, "cmo", [128, 1, 64])
                cmb = sb(pb, "cmb", [128, 1, 64], BF16)
                csc = loadc(pb, "cs_cmp")
                ovl = loadc(pb, "ovl")
                cnt = 0
                for j in range(2):
                    srcT = kcT if j == 0 else vcT
                    lo = srcT[:, 0:T].rearrange("d (n s) -> d n s", s=16)
                    hi = srcT[:, 16:T].rearrange("d (n s) -> d n s", s=16)
                    for l_ in range(32):
                        sv = lo[:, 0:127, l_] if l_ < 16 else hi[:, 0:127, l_ - 16]
                        P.ts("dve" if l_ % 2 else "pool", blk[:, l_, :], sv,
                             vec[0:64, ope + j * 32 + l_:ope + j * 32 + l_ + 1], ALU.add, r=[srcT, vec], w=[blk])
                    P.dma(w2s[:, :, :], nsa_w2[l, j].rearrange("(c p) d -> p c d", p=128), r=[w2s], w=[w2s])
                    P.cp("dve", w2b[:, :, :], w2s[:, :, :], r=[w2s], w=[w2b])
                    for lg in range(4):
                        ws, wb = w1s[cnt % 2], w1b[cnt % 2]
                        cnt += 1
                        P.dma(ws[:, :, :], nsa_w1[l, j, lg * 512:(lg + 1) * 512, :].rearrange("(l d) c -> d l c", d=64),
                              r=[ws], w=[ws])
                        P.cp("pool" if lg % 2 else "dve", wb[:, :, :], ws[:, :, :], r=[ws], w=[wb])
                        for lq in range(8):
                            l_ = lg * 8 + lq
                            for cc in range(2):
                                P.mm(PS[2 + cc][:, 0:127], wb[:, lq, cc * 128:(cc + 1) * 128], blk[:, l_, :],
                                     start=(l_ == 0), stop=(l_ == 31), r=[wb, blk], w=[PS[2 + cc]])
                    for cc in range(2):
                        pz = PS[2 + cc][:, 0:127]
                        P.act(gx2[:, :], pz, AF.Square, r=[PS[2 + cc]], w=[gx2])
                        P.ts("dve", gx2[:, :], gx2[:, :], 0.044715, ALU.mult, 1.0, ALU.add, r=[gx2], w=[gx2])
                        P.tt("dve", gu[:, :], gx2[:, :], pz, ALU.mult, r=[gx2, PS[2 + cc]], w=[gu])
                        P.act(gu[:, :], gu[:, :], AF.Sigmoid, r=[gu], w=[gu], scale=1.5957691216057308)
                        P.tt("dve", GT_[:, cc, :], gu[:, :], pz, ALU.mult, r=[gu, PS[2 + cc]], w=[GT_])
                    for cc in range(2):
                        P.mm(PS[4][0:127, 0:64], GT_[:, cc, :], w2b[:, cc, :], start=(cc == 0), stop=(cc == 1),
                             r=[GT_, w2b], w=[PS[4]])
                    if j == 0:
                        P.cp("act", cmo[0:127, 0, :], PS[4][0:127, 0:64], r=[PS[4]], w=[cmo])
                        src_b, dst_b = cmo, cmb
                        norm_rope(cmo[0:127, 0:1, :], 1, gK[0:127, :].unsqueeze(1),
                                  csc[0:127, 0:32].unsqueeze(1), csc[0:127, 32:64].unsqueeze(1), cmb[0:127, 0:1, :], tmp, rows=127, tab=csc)
                        P.tr(PSB[5][0:64, 0:127], cmb[0:127, 0, :], identb[0:127, 0:127], r=[cmb, identb], w=[PS[5]])
                        P.cp("dve", kcmpT[:, 0:127], PSB[5][0:64, 0:127], r=[PS[5]], w=[kcmpT])
                    else:
                        P.cp("act", vcp[0:127, 0:64], PS[4][0:127, 0:64], r=[PS[4]], w=[vcp])
                        P.cp("dve", vcp[0:127, 64:97], ovl[0:127, :], r=[ovl], w=[vcp])
                P.flush()
            qT = sb(pa, "qTn", [64, 2, 512], BF16, nsub=2)
            gt = sb(pa, "gt", [128, 12])
            pc = sb(pa, "pc", [128, 512], BF16)
            pT = sb(pa, "pTn", [128, 2, 512], BF16, nsub=2)
            oc = sb(pa, "oc", [128, 4, 64])
            ocb = sb(pa, "ocb", [128, 256], BF16)
            imp = sb(pa, "imp", [128, 32])
            imw = sb(pa, "imw", [128, 32])
            m8a = sb(pa, "m8a", [128, 8])
            m8b = sb(pa, "m8b", [128, 8])
            nmb = sb(pa, "nmb", [128, 32], BF16)
            nmT4 = sb(pa, "nmT4", [32, 4, 128], BF16)
            rec = sb(pa, "recn", [128, 4])
            coef = sb(pa, "coef", [128, 4])

            def finish(gcol, first):
                for h in range(4):
                    acc = PS[4 + h]
                    if first:
                        P.ts("dve", rec[:, h:h + 1], acc[:, 64:65], 1e-30, ALU.max, r=[acc], w=[rec])
                        P.recip(rec[:, h:h + 1], rec[:, h:h + 1], r=[rec], w=[rec])
                    else:
                        P.recip(rec[:, h:h + 1], acc[:, 64:65], r=[acc], w=[rec])
                    P.tt("dve", coef[:, h:h + 1], rec[:, h:h + 1], gt[:, 3 * h + gcol:3 * h + gcol + 1], ALU.mult,
                         r=[rec, gt], w=[coef])
                    if first:
                        P.ts("dve", oc[:, h, :], acc[:, 0:64], coef[:, h:h + 1], ALU.mult, r=[acc, coef], w=[oc])
                    else:
                        P.stt(oc[:, h, :], acc[:, 0:64], coef[:, h:h + 1], oc[:, h, :], ALU.mult, ALU.add,
                              r=[acc, coef, oc], w=[oc])

            for i in range(NT):
                tsl = slice(i * 128, (i + 1) * 128)
                b = i % 2
                for kc in range(8):
                    P.mm(PS[0][:, 0:256], hT[:, kc, tsl], wC[:, kc, 0:256], start=(kc == 0), stop=(kc == 7),
                         r=[(hT, i), wC], w=[PS[0]])
                for kc in range(8):
                    P.mm(PS[1][:, 0:12], hT[:, kc, tsl], wC[:, kc, 640:652], start=(kc == 0), stop=(kc == 7),
                         r=[(hT, i), wC], w=[PS[1]])
                P.cp("act", rwq[:, :, :], PS[0][:, 0:256].rearrange("p (g d) -> p g d", g=4), r=[PS[0]], w=[rwq])
                P.act(gt[:, :], PS[1][:, 0:12], AF.Sigmoid, r=[PS[1]], w=[gt])
                src_b, dst_b = rwq, roq
                norm_rope(rwq[:, :, :], 4, gQ[:, :].rearrange("p (g d) -> p g d", g=4),
                          cs[:, i * 64:i * 64 + 32].unsqueeze(1).to_broadcast([128, 4, 32]),
                          cs[:, i * 64 + 32:i * 64 + 64].unsqueeze(1).to_broadcast([128, 4, 32]), roq[:, :, :], tmp)
                for g in range(4):
                    P.tr(PSB[0][0:64, 512 + g * 128:512 + (g + 1) * 128], roq[:, g, :], identb[:, :], r=[roq, identb], w=[PS[0]])
                P.cp("dve", qT[:, b, :], PSB[0][0:64, 512:1024], r=[PS[0]], w=[(qT, b)])
                nv = 8 * i + 7
                P.mm(PS[2][0:nv, :], kcmpT[:, 0:nv], qT[:, b, :], r=[kcmpT, (qT, b)], w=[PS[2]])
                P.act(pc[0:nv, :], PS[2][0:nv, :], AF.Exp, r=[PS[2]], w=[pc], scale=0.125)
                P.tt("dve", pc[0:nv, :].rearrange("p (h t) -> p h t", h=4), pc[0:nv, :].rearrange("p (h t) -> p h t", h=4),
                     validb[0:nv, tsl].unsqueeze(1).to_broadcast([nv, 4, 128]), ALU.mult, r=[pc, validb], w=[pc])
                for h in range(4):
                    P.mm(PS[4 + h][:, 0:97], pc[0:nv, h * 128:(h + 1) * 128], vcp[0:nv, :], r=[pc, vcp], w=[PS[4 + h]])
                finish(0, True)
                for h in range(4):
                    acc = PS[4 + h]
                    if h == 0:
                        P.ts("dve", imp[:, :], acc[:, 65:97], rec[:, 0:1], ALU.mult, r=[acc, rec], w=[imp])
                    else:
                        P.stt(imp[:, :], acc[:, 65:97], rec[:, h:h + 1], imp[:, :], ALU.mult, ALU.add,
                              r=[acc, rec, imp], w=[imp])
                P.tt("dve", imp[:, :], imp[:, :], addm[:, i * 32:(i + 1) * 32], ALU.add, r=[imp, addm], w=[imp])
                P.op("dve", lambda e: e.max(out=m8a[:, :], in_=imp[:, :]), r=[imp], w=[m8a])
                P.op("dve", lambda e: e.match_replace(out=imw[:, :], in_to_replace=m8a[:, :], in_values=imp[:, :],
                                                      imm_value=-3.0e38), r=[imp, m8a], w=[imw])
                P.op("dve", lambda e: e.max(out=m8b[:, :], in_=imw[:, :]), r=[imw], w=[m8b])
                P.ts("dve", nmb[:, :], imp[:, :], m8b[:, 7:8], ALU.is_lt, -BIG, ALU.mult, r=[imp, m8b], w=[nmb])
                P.tr(PSB[1][0:32, 0:128], nmb[:, :], identb[:, :], r=[nmb, identb], w=[PS[1]])
                P.cp("dve", nmT4[:, :, :], PSB[1][0:32, 0:128].unsqueeze(1).to_broadcast([32, 4, 128]), r=[PS[1]], w=[nmT4])
                for j in range(i + 1):
                    bank = PS[2 + j % 2]
                    jsl = slice(j * 128, (j + 1) * 128)
                    P.mm(bank[:, :], ksT[:, jsl], qT[:, b, :], start=True, stop=False, r=[(ksT, j), (qT, b)], w=[bank])
                    P.mm(bank[:, :], efb[0:32, jsl], nmT4[:, :, :].rearrange("p h t -> p (h t)"), start=False, stop=(j != i),
                         r=[efb, nmT4], w=[bank])
                    if j == i:
                        P.mm(bank[:, :], identb[:, :], negu4[:, :], start=False, stop=True, r=[identb, negu4], w=[bank])
                    P.act(pT[:, j % 2, :], bank[:, :], AF.Exp, r=[bank], w=[(pT, j % 2)], scale=0.125)
                    for h in range(4):
                        P.mm(PS[4 + h][:, 0:65], pT[:, j % 2, h * 128:(h + 1) * 128], Vs[:, j, :],
                             start=(j == 0), stop=(j == i), r=[(pT, j % 2), (Vs, j)], w=[PS[4 + h]])
                finish(1, False)
                j0 = max(0, i - 4)
                for j in range(j0, i + 1):
                    bank = PS[2 + j % 2]
                    jsl = slice(j * 128, (j + 1) * 128)
                    extra = (j == i) or (j == i - 4)
                    P.mm(bank[:, :], kwT[:, jsl], qT[:, b, :], start=True, stop=not extra, r=[(kwT, j), (qT, b)], w=[bank])
                    if j == i:
                        P.mm(bank[:, :], identb[:, :], negu4[:, :], start=False, stop=True, r=[identb, negu4], w=[bank])
                    elif j == i - 4:
                        P.mm(bank[:, :], identb[:, :], negl4[:, :], start=False, stop=True, r=[identb, negl4], w=[bank])
                    P.act(pT[:, j % 2, :], bank[:, :], AF.Exp, r=[bank], w=[(pT, j % 2)], scale=0.125)
                    for h in range(4):
                        P.mm(PS[4 + h][:, 0:65], pT[:, j % 2, h * 128:(h + 1) * 128], Vw[:, j, :],
                             start=(j == j0), stop=(j == i), r=[(pT, j % 2), (Vw, j)], w=[PS[4 + h]])
                finish(2, False)
                P.cp("act", ocb[:, :], oc[:, :, :].rearrange("p h d -> p (h d)"), r=[oc], w=[ocb])
                for c in range(2):
                    P.tr(PSB[1][:, 512 + c * 128:512 + (c + 1) * 128], ocb[:, c * 128:(c + 1) * 128], identb[:, :],
                         r=[ocb, identb], w=[PS[1]])
                P.cp("dve", ocT[:, :, tsl], PSB[1][:, 512:768].rearrange("p (c t) -> p c t", c=2), r=[PS[1]], w=[(ocT, i)])
            P.flush()
        outproj_add(ocT, 2, 768, 16, l, "C")


def _pack_inputs(inp):
    L = DEPTH
    vecs = np.zeros((L, 128, NVC), np.float32)
    rows = np.zeros((L, NRW), np.float32)
    for l in range(L):
        def put(name, arr):
            o, w = VC[name]
            vecs[l, :, o:o + w] = arr
        put("ada_b", _col(inp["ada_b"][l]))
        put("n1g", _col(inp["norm1_g"][l]))
        put("n2g", _col(inp["norm2_g"][l]))
        put("mu", _col(inp["rwkv_mu"][l]))
        put("a0", _col(inp["rwkv_a0"][l]))
        put("k_k", _col(inp["rwkv_k_k"][l]))
        put("k_a", _col(inp["rwkv_k_a"][l]))
        put("r_k", _col(np.asarray(inp["rwkv_r_k"][l]).reshape(-1)))
        pe = np.asarray(inp["nsa_pe"][l], np.float32)
        o, w = VC["pe"]
        vecs[l, 0:64, o:o + 64] = pe.transpose(2, 0, 1).reshape(64, 64)

        def prow(name, arr):
            o, w = RW[name]
            rows[l, o:o + w] = np.asarray(arr, np.float32).reshape(-1)
        prow("dsa_g", np.concatenate([np.tile(inp["dsa_q_g"][l], 4), inp["dsa_k_g"][l]]))
        prow("nsa_qg", np.tile(inp["nsa_q_g"][l], 4))
        prow("nsa_kg", inp["nsa_k_g"][l])
        prow("ln_w", inp["rwkv_ln_w"][l])
        prow("ln_b", inp["rwkv_ln_b"][l])
        prow("w0", inp["rwkv_w0"][l])
    return vecs, rows


_SHARED = ("ada_w", "w_in", "w_out", "ffn_wi", "ffn_wo", "rwkv_w2", "rwkv_a2", "rwkv_g2", "nsa_w1", "nsa_w2")


def make_in_maps(inp):
    vecs, rows = _pack_inputs(inp)
    shared = {k: np.ascontiguousarray(np.asarray(inp[k], np.float32)) for k in _SHARED}
    maps = []
    for b in range(8):
        m = dict(shared)
        m["x"] = np.ascontiguousarray(np.asarray(inp["x"][b], np.float32))
        m["ccol"] = _col(inp["c"][b])
        m["cfd"] = CONSTS
        m["vec"] = vecs
        m["row"] = rows
        maps.append(m)
    return maps


def kernel(**inputs):
    nc = build()
    maps = make_in_maps(inputs)
    res = run_bass_kernel_spmd(nc, maps, core_ids=list(range(8)))
    return np.stack([np.asarray(r["y"], np.float32) for r in res.results], axis=0)
```
